# Optimizing a Trainium2 kernel written in Bass

```python
import math
import jax, jax.numpy as jnp
from jax import lax
import numpy as np

D_MODEL = 1024
BATCH = 8
SEQ = 4096
DEPTH = 4

PLE_DIM = 256
D_FF = 4 * D_MODEL
N_EVEN = (DEPTH + 1) // 2
N_ODD = DEPTH // 2
EPS = 1e-6
DIFF_DH = 64
DIFF_HEADS = D_MODEL // 256
DIFF_WIDTH = DIFF_HEADS * 2 * DIFF_DH
SWA_DH = 64
SWA_WIDTH = D_MODEL - DIFF_WIDTH
SWA_HEADS = SWA_WIDTH // SWA_DH
DILATED_CONFIGS = ((128, 1), (512, 4), (2048, 16))
Q_BLOCK = 128
SSM_GROUP_CH = 16
SSM_STATE = 64
SSM_WIDTH = D_MODEL // 2
SSM_GROUPS = SSM_WIDTH // SSM_GROUP_CH
CONV_WIDTH = D_MODEL - SSM_WIDTH
CONV_K = 3
EVEN_IN = 3 * DIFF_WIDTH + 3 * SWA_WIDTH
ODD_IN = SSM_WIDTH + 3 * CONV_WIDTH

kernel_name = 'hybrid_diffattn_dilated_s5_shortconv_trunk'


def rms_norm(x, g, eps=EPS):
    xf = x.astype(jnp.float32)
    y = xf * lax.rsqrt(jnp.mean(xf * xf, axis=-1, keepdims=True) + eps) * g.astype(jnp.float32)
    return y.astype(x.dtype)


def diff_attention(q, k, v, lam, sub_gain, lam_init):
    b, s, h = q.shape[:3]
    nb = s // Q_BLOCK
    scale = DIFF_DH ** -0.5
    kf = k.astype(jnp.float32)
    vf = v.astype(jnp.float32)
    qb = jnp.moveaxis(q.astype(jnp.float32).reshape(b, nb, Q_BLOCK, h, 2, DIFF_DH), 1, 0)
    kpos = jnp.arange(s)

    def block(args):
        q_blk, n = args
        sc = jnp.einsum('bqhcd,bkhcd->bhcqk', q_blk, kf) * scale
        qpos = n * Q_BLOCK + jnp.arange(Q_BLOCK)
        sc = jnp.where(kpos[None, :] <= qpos[:, None], sc, -jnp.inf)
        a = jax.nn.softmax(sc, axis=-1)
        w = a[:, :, 0] - lam * a[:, :, 1]
        return jnp.einsum('bhqk,bkhe->bqhe', w, vf)

    o = lax.map(block, (qb, jnp.arange(nb)))
    o = jnp.moveaxis(o, 0, 1).reshape(b, s, h, 2 * DIFF_DH)
    o = o * lax.rsqrt(jnp.mean(o * o, axis=-1, keepdims=True) + 1e-5) * sub_gain.astype(jnp.float32)
    o = o * (1.0 - lam_init)
    return o.reshape(b, s, h * 2 * DIFF_DH)


def strided_window_attention(q, k, v, n_back, dil):
    b, s, h, dh = q.shape
    L = s // dil
    blk = n_back
    nb = -(-L // blk)
    pad_end = nb * blk - L

    def sub(t):
        return t.astype(jnp.float32).reshape(b, L, dil, h, dh)

    qs = jnp.pad(sub(q), ((0, 0), (0, pad_end), (0, 0), (0, 0), (0, 0))).reshape(b, nb, blk, dil, h, dh)

    def band(t):
        tp = jnp.pad(sub(t), ((0, 0), (blk, pad_end), (0, 0), (0, 0), (0, 0))).reshape(b, nb + 1, blk, dil, h, dh)
        return jnp.concatenate([tp[:, :-1], tp[:, 1:]], axis=2)

    kb, vb = band(k), band(v)
    sc = jnp.einsum('bnqrhd,bnkrhd->bnrhqk', qs, kb) * dh ** -0.5
    qi = jnp.arange(blk)[:, None]
    kk = jnp.arange(2 * blk)[None, :]
    dist = blk + qi - kk
    kpos = (jnp.arange(nb)[:, None, None] - 1) * blk + kk[None]
    valid = (dist >= 0) & (dist <= n_back) & (kpos >= 0)
    sc = jnp.where(valid[None, :, None, None], sc, -jnp.inf)
    m = jnp.max(sc, axis=-1)
    e = jnp.exp(sc - m[..., None])
    den = jnp.sum(e, axis=-1)
    o = jnp.einsum('bnrhqk,bnkrhd->bnqrhd', e, vb) / jnp.moveaxis(den, -1, 2)[..., None]
    o = o.reshape(b, nb * blk, dil, h, dh)[:, :L].reshape(b, s, h, dh)
    m = jnp.moveaxis(m, -1, 2).reshape(b, nb * blk, dil, h)[:, :L].reshape(b, s, h)
    den = jnp.moveaxis(den, -1, 2).reshape(b, nb * blk, dil, h)[:, :L].reshape(b, s, h)
    return o, m, den


def dilated_attention(q, k, v):
    b, s, h, dh = q.shape
    outs = [strided_window_attention(q, k, v, w // d, d) for (w, d) in DILATED_CONFIGS]
    m_all = jnp.max(jnp.stack([m for (_, m, _) in outs]), axis=0)
    wts = [den * jnp.exp(m - m_all) for (_, m, den) in outs]
    num = sum(wt[..., None] * o for wt, (o, _, _) in zip(wts, outs))
    tot = sum(wts)
    return (num / tot[..., None]).reshape(b, s, h * dh)


def attention_mixer(h, w_in, w_out, lq1, lk1, lq2, lk2, sub_gain, lam_init):
    b, s, _ = h.shape
    proj = h @ w_in
    cuts = list(np.cumsum([DIFF_WIDTH] * 3 + [SWA_WIDTH] * 2))
    aq, ak, av, bq, bk, bv = jnp.split(proj, cuts, axis=-1)
    f32 = jnp.float32
    lam = (jnp.exp(jnp.sum(lq1.astype(f32) * lk1.astype(f32))) - jnp.exp(jnp.sum(lq2.astype(f32) * lk2.astype(f32)))
           + lam_init)
    ya = diff_attention(aq.reshape(b, s, DIFF_HEADS, 2, DIFF_DH), ak.reshape(b, s, DIFF_HEADS, 2, DIFF_DH),
                        av.reshape(b, s, DIFF_HEADS, 2 * DIFF_DH), lam, sub_gain, lam_init)
    yb = dilated_attention(bq.reshape(b, s, SWA_HEADS, SWA_DH), bk.reshape(b, s, SWA_HEADS, SWA_DH),
                           bv.reshape(b, s, SWA_HEADS, SWA_DH))
    y = jnp.concatenate([ya, yb], axis=-1).astype(h.dtype)
    return y @ w_out


def s5_mixer(u, lam_re, lam_im, log_dt, b_re, b_im, c_re, c_im, d_skip, w_glu):
    f32 = jnp.float32
    bsz, s, _ = u.shape
    uf = u.astype(f32).reshape(bsz, s, SSM_GROUPS, SSM_GROUP_CH)
    lr, li = lam_re.astype(f32), lam_im.astype(f32)
    dt = jnp.exp(log_dt.astype(f32))[:, None]
    mag = jnp.exp(lr * dt)
    abar_re, abar_im = mag * jnp.cos(li * dt), mag * jnp.sin(li * dt)
    den = lr * lr + li * li
    nr, ni = abar_re - 1.0, abar_im
    coef_re = (nr * lr + ni * li) / den
    coef_im = (ni * lr - nr * li) / den
    br, bi = b_re.astype(f32), b_im.astype(f32)
    bb_re = coef_re[..., None] * br - coef_im[..., None] * bi
    bb_im = coef_re[..., None] * bi + coef_im[..., None] * br
    bu_re = jnp.einsum('bsgc,gpc->bsgp', uf, bb_re)
    bu_im = jnp.einsum('bsgc,gpc->bsgp', uf, bb_im)
    a_re = jnp.broadcast_to(abar_re, (1, s, SSM_GROUPS, SSM_STATE))
    a_im = jnp.broadcast_to(abar_im, (1, s, SSM_GROUPS, SSM_STATE))

    def combine(e1, e2):
        a1r, a1i, b1r, b1i = e1
        a2r, a2i, b2r, b2i = e2
        return (a1r * a2r - a1i * a2i, a1r * a2i + a1i * a2r,
                a2r * b1r - a2i * b1i + b2r, a2r * b1i + a2i * b1r + b2i)

    _, _, xr, xi = lax.associative_scan(combine, (a_re, a_im, bu_re, bu_im), axis=1)
    y = (jnp.einsum('bsgp,gcp->bsgc', xr, c_re.astype(f32)) - jnp.einsum('bsgp,gcp->bsgc', xi, c_im.astype(f32))
         + d_skip.astype(f32) * uf)
    y = jax.nn.gelu(y.reshape(bsz, s, SSM_WIDTH))
    return y * jax.nn.sigmoid(y @ w_glu.astype(f32))


def short_conv_mixer(gb, gc, xt, conv_w):
    s = xt.shape[1]
    z = gc * xt
    zp = jnp.pad(z, ((0, 0), (CONV_K - 1, 0), (0, 0)))
    y = sum(conv_w[j] * zp[:, CONV_K - 1 - j: CONV_K - 1 - j + s] for j in range(CONV_K))
    return gb * y


def recurrent_conv_mixer(h, w_in, w_out, lam_re, lam_im, log_dt, b_re, b_im, c_re, c_im, d_skip, w_glu, conv_w):
    proj = h @ w_in
    u, gb, gc, xt = jnp.split(proj, [SSM_WIDTH, SSM_WIDTH + CONV_WIDTH, SSM_WIDTH + 2 * CONV_WIDTH], axis=-1)
    yc = s5_mixer(u, lam_re, lam_im, log_dt, b_re, b_im, c_re, c_im, d_skip, w_glu)
    yd = short_conv_mixer(gb, gc, xt, conv_w)
    y = jnp.concatenate([yc.astype(h.dtype), yd.astype(h.dtype)], axis=-1)
    return y @ w_out


def setup_inputs(seed: int = 0) -> dict:
    key = jax.random.key(seed)
    ks = iter(jax.random.split(key, 32))

    def nrm(shape, scale):
        return jax.random.normal(next(ks), shape, jnp.float32) * scale

    def gain(shape):
        return 1.0 + nrm(shape, 0.02)

    return {
        'x': nrm((BATCH, SEQ, D_MODEL), 1.0),
        'p': nrm((DEPTH, BATCH, SEQ, PLE_DIM), 1.0),
        'norm_mix': gain((DEPTH, D_MODEL)),
        'norm_mlp': gain((DEPTH, D_MODEL)),
        'norm_ple': gain((DEPTH, D_MODEL)),
        'w_mlp_in': nrm((DEPTH, D_MODEL, D_FF), D_MODEL ** -0.5),
        'w_mlp_out': nrm((DEPTH, D_FF, D_MODEL), D_FF ** -0.5),
        'w_ple_proj': nrm((DEPTH, PLE_DIM, D_MODEL), PLE_DIM ** -0.5),
        'w_ple_gate': nrm((DEPTH, D_MODEL, D_MODEL), D_MODEL ** -0.5),
        'attn_w_in': nrm((N_EVEN, D_MODEL, EVEN_IN), D_MODEL ** -0.5),
        'attn_w_out': nrm((N_EVEN, DIFF_WIDTH + SWA_WIDTH, D_MODEL), D_MODEL ** -0.5),
        'diff_lq1': nrm((N_EVEN, DIFF_DH), 0.1),
        'diff_lk1': nrm((N_EVEN, DIFF_DH), 0.1),
        'diff_lq2': nrm((N_EVEN, DIFF_DH), 0.1),
        'diff_lk2': nrm((N_EVEN, DIFF_DH), 0.1),
        'diff_sub_gain': gain((N_EVEN, 2 * DIFF_DH)),
        'rc_w_in': nrm((N_ODD, D_MODEL, ODD_IN), D_MODEL ** -0.5),
        'rc_w_out': nrm((N_ODD, SSM_WIDTH + CONV_WIDTH, D_MODEL), D_MODEL ** -0.5),
        'ssm_lambda_re': -0.5 + nrm((N_ODD, SSM_GROUPS, SSM_STATE), 0.01),
        'ssm_lambda_im': math.pi * jnp.arange(SSM_STATE, dtype=jnp.float32) + nrm((N_ODD, SSM_GROUPS, SSM_STATE), 0.01),
        'ssm_log_dt': jax.random.uniform(next(ks), (N_ODD, SSM_GROUPS), jnp.float32, math.log(1e-3), math.log(1e-1)),
        'ssm_b_re': nrm((N_ODD, SSM_GROUPS, SSM_STATE, SSM_GROUP_CH), (2 * SSM_GROUP_CH) ** -0.5),
        'ssm_b_im': nrm((N_ODD, SSM_GROUPS, SSM_STATE, SSM_GROUP_CH), (2 * SSM_GROUP_CH) ** -0.5),
        'ssm_c_re': nrm((N_ODD, SSM_GROUPS, SSM_GROUP_CH, SSM_STATE), (2 * SSM_STATE) ** -0.5),
        'ssm_c_im': nrm((N_ODD, SSM_GROUPS, SSM_GROUP_CH, SSM_STATE), (2 * SSM_STATE) ** -0.5),
        'ssm_d': nrm((N_ODD, SSM_GROUPS, SSM_GROUP_CH), 1.0),
        'ssm_w_glu': nrm((N_ODD, SSM_WIDTH, SSM_WIDTH), SSM_WIDTH ** -0.5),
        'conv_w': nrm((N_ODD, CONV_K, CONV_WIDTH), CONV_K ** -0.5),
        'norm_final': gain((D_MODEL,)),
    }


def reference(x, p, norm_mix, norm_mlp, norm_ple, w_mlp_in, w_mlp_out, w_ple_proj, w_ple_gate,
              attn_w_in, attn_w_out, diff_lq1, diff_lk1, diff_lq2, diff_lk2, diff_sub_gain,
              rc_w_in, rc_w_out, ssm_lambda_re, ssm_lambda_im, ssm_log_dt, ssm_b_re, ssm_b_im,
              ssm_c_re, ssm_c_im, ssm_d, ssm_w_glu, conv_w, norm_final):
    h = x
    for i in range(DEPTH):
        hn = rms_norm(h, norm_mix[i])
        if i % 2 == 0:
            e = i // 2
            lam_init = 0.8 - 0.6 * math.exp(-0.3 * i)
            y = attention_mixer(hn, attn_w_in[e], attn_w_out[e], diff_lq1[e], diff_lk1[e], diff_lq2[e],
                                diff_lk2[e], diff_sub_gain[e], lam_init)
        else:
            o = i // 2
            y = recurrent_conv_mixer(hn, rc_w_in[o], rc_w_out[o], ssm_lambda_re[o], ssm_lambda_im[o],
                                     ssm_log_dt[o], ssm_b_re[o], ssm_b_im[o], ssm_c_re[o], ssm_c_im[o],
                                     ssm_d[o], ssm_w_glu[o], conv_w[o])
        h = h + y.astype(h.dtype)
        hn = rms_norm(h, norm_mlp[i])
        h = h + jnp.square(jax.nn.relu(hn @ w_mlp_in[i])) @ w_mlp_out[i]
        hn = rms_norm(h, norm_ple[i])
        h = h + (p[i] @ w_ple_proj[i]) * jax.nn.sigmoid(hn @ w_ple_gate[i])
    return rms_norm(h, norm_final)
```

```python
import numpy as np
from contextlib import ExitStack
import concourse.bass as bass
import concourse.mybir as mybir
from concourse.bass_utils import run_bass_kernel_spmd

F32 = mybir.dt.float32
BF16 = mybir.dt.bfloat16
ALU = mybir.AluOpType
AF = mybir.ActivationFunctionType
AX = mybir.AxisListType

COMPUTE = ('pe', 'act', 'dve', 'pool')


class Buf:
    def __init__(self, name, t=None):
        self.name = name
        self.t = t
        self.last_w = None
        self.readers = []
        self.sem = None
        self.dma_cnt = 0


class Op:
    __slots__ = ('eng', 'fn', 'is_dma', 'deps', 'flag', 'cnt', 'dbuf', 'dval', 'idx', 'sem')

    def __init__(self, eng, fn, is_dma):
        self.eng = eng
        self.fn = fn
        self.is_dma = is_dma
        self.deps = []
        self.flag = False
        self.cnt = 0
        self.dbuf = None
        self.dval = 0


class Prog:
    def __init__(self, nc):
        self.nc = nc
        self.ops = []
        self.finals = []
        self.n_sb = 0

    uid = [0]

    def sb(self, es, name, shape, dtype):
        Prog.uid[0] += 1
        name = "%s_%d" % (name, Prog.uid[0])
        t = es.enter_context(self.nc.sbuf_tensor(name, list(shape), dtype))
        return Buf(name, t)

    def ps(self, es, name, shape, dtype):
        Prog.uid[0] += 1
        name = "%s_%d" % (name, Prog.uid[0])
        t = es.enter_context(self.nc.psum_tensor(name, list(shape), dtype))
        return Buf(name, t)

    def view(self, name, t):
        return Buf(name, t)

    def _add(self, op, reads, writes):
        deps = {}
        for b in reads:
            if b.last_w is not None:
                deps[id(b.last_w)] = (b.last_w, 'raw')
        for b in writes:
            if b.last_w is not None:
                deps[id(b.last_w)] = (b.last_w, 'waw')
            for r in b.readers:
                if id(r) not in deps:
                    deps[id(r)] = (r, 'war')
        for b in reads:
            b.readers.append(op)
        for b in writes:
            b.last_w = op
            b.readers = []
        for p, kind in deps.values():
            if p is op:
                continue
            if not p.is_dma:
                if p.eng == op.eng and not op.is_dma:
                    if p.eng == 'pe':
                        continue
                p.flag = True
            op.deps.append(p)
        op.idx = len(self.ops)
        self.ops.append(op)
        return op

    def op(self, eng, fn, reads, writes):
        return self._add(Op(eng, fn, False), reads, writes)

    def pe(self, fn, reads, writes):
        return self.op('pe', fn, reads, writes)

    def act(self, fn, reads, writes):
        return self.op('act', fn, reads, writes)

    def dve(self, fn, reads, writes):
        return self.op('dve', fn, reads, writes)

    def pool(self, fn, reads, writes):
        return self.op('pool', fn, reads, writes)

    def dma(self, q, out, in_, reads, writes, sbuf=None, final=False, **kw):
        if sbuf is None:
            cands = [b for b in list(reads) + list(writes) if getattr(b, 'is_sb', True) and b.t is not None]
            sbuf = cands[0]
        op = Op(q, (lambda e: e.dma_start(out=out, in_=in_, **kw)), True)
        op.dbuf = sbuf
        sbuf.dma_cnt += 16
        op.dval = sbuf.dma_cnt
        self._add(op, reads, writes)
        if final:
            self.finals.append(op)
        return op

    def emit(self):
        nc = self.nc
        with ExitStack() as es:
            EPOCH = 1000
            esem = {}
            allsems = []
            for op in self.ops:
                if op.is_dma and op.dbuf.sem is None:
                    Prog.uid[0] += 1
                    op.dbuf.sem = nc.alloc_semaphore(name='d_%d_%s' % (Prog.uid[0], op.dbuf.name))
                    allsems.append(op.dbuf.sem)
            cnt = {e: 0 for e in COMPUTE}
            for op in self.ops:
                if not op.is_dma and op.flag:
                    ep = cnt[op.eng] // EPOCH
                    if (op.eng, ep) not in esem:
                        Prog.uid[0] += 1
                        esem[(op.eng, ep)] = nc.alloc_semaphore(name='s_%s_%d_%d' % (op.eng, ep, Prog.uid[0]))
                        allsems.append(esem[(op.eng, ep)])
                    op.sem = esem[(op.eng, ep)]
                    op.cnt = cnt[op.eng] % EPOCH + 1
                    cnt[op.eng] += 1
            per_eng = {e: [] for e in ('pe', 'act', 'dve', 'pool', 'sp')}
            for op in self.ops:
                per_eng[op.eng].append(op)
            finals = self.finals
            self.n_inst = {e: len(v) for e, v in per_eng.items()}

            def run(eng_name, e):
                waited = {}
                for op in per_eng[eng_name]:
                    need = {}
                    for p in op.deps:
                        if p.is_dma:
                            s, v = p.dbuf.sem, p.dval
                        else:
                            s, v = p.sem, p.cnt
                        k = id(s)
                        if k not in need or need[k][1] < v:
                            need[k] = (s, v)
                    for k, (s, v) in need.items():
                        if waited.get(k, 0) >= v:
                            continue
                        waited[k] = v
                        e.wait_ge(s, v)
                    ins = op.fn(e)
                    if op.is_dma:
                        ins.then_inc(op.dbuf.sem, 16)
                    elif op.flag:
                        ins.then_inc(op.sem, 1)
                if eng_name == 'sp':
                    need = {}
                    for p in finals:
                        s, v = p.dbuf.sem, p.dval
                        if id(s) not in need or need[id(s)][1] < v:
                            need[id(s)] = (s, v)
                    for s, v in need.values():
                        e.wait_ge(s, v)

            with nc.Block() as block:
                @block.sync
                def _(e):
                    run('sp', e)

                @block.tensor
                def _(e):
                    run('pe', e)

                @block.scalar
                def _(e):
                    run('act', e)

                @block.vector
                def _(e):
                    run('dve', e)

                @block.gpsimd
                def _(e):
                    run('pool', e)
            nc.all_engine_barrier()
            nc.clear_and_free_semaphores(allsems)
            nc.all_engine_barrier()


def bcast_rows(ap, n):
    return ap.partition_broadcast(n)


def make_identity(P, ident):
    t = ident.t
    P.pool(lambda e: e.memset(t[:], 1.0), [], [ident])
    P.pool(lambda e: e.affine_select(t[:], t[:], pattern=[[-1, 128]], compare_op=ALU.is_equal,
                                    fill=0.0, base=0, channel_multiplier=1), [ident], [ident])


S = 4096
D = 1024
NT = S // 128
EPS = 1e-6


class Ctx:
    pass


def rot(lst, i):
    return lst[i % len(lst)]


def load_w_bf16(P, wt, w_ap, nk, q='pool'):
    if not hasattr(P, 'wtok'):
        P.wtok = [Buf('wtok%d' % j) for j in range(3)]
        P.nw = 0
    for k in range(nk):
        tok = P.wtok[P.nw % 3]
        P.nw += 1
        P.dma(q, wt.t[:, k, :], w_ap[k * 128:(k + 1) * 128, :], reads=[], writes=[wt, tok], sbuf=wt)


def emit_norm_T(P, C, ht, gt, hn, pT, hnT_dst, hnT_buf, ss, rs, junk, evac_eng):
    P.act(lambda e: e.activation(junk.t[:], ht.t[:], AF.Square, accum_out=ss.t[:]), [ht], [junk, ss])
    P.act(lambda e: e.activation(rs.t[:], ss.t[:], AF.Sqrt, bias=EPS, scale=1.0 / D), [ss], [rs])
    P.dve(lambda e: e.reciprocal(rs.t[:], rs.t[:]), [rs], [rs])
    P.dve(lambda e: e.scalar_tensor_tensor(hn.t[:], ht.t[:], rs.t[:], gt.t[:], ALU.mult, ALU.mult), [ht, rs, gt], [hn])
    for k in range(8):
        P.pe(lambda e, k=k: e.transpose(pT.t[:, k * 128:(k + 1) * 128], hn.t[:, k * 128:(k + 1) * 128], C.ident.t[:]),
             [hn, C.ident], [pT])
    src = pT.t[:, :].rearrange("p (k t) -> p k t", k=8)
    if evac_eng == 'act':
        P.act(lambda e: e.copy(hnT_dst, src), [pT], [hnT_buf])
    else:
        P.dve(lambda e: e.tensor_copy(hnT_dst, src), [pT], [hnT_buf])


def pass_A(nc, C, li, src_ap):
    even = (li % 2 == 0)
    NW = 3072 if even else 2048
    w_ap = (C.attn_w_in if even else C.rc_w_in)[li // 2]
    if even:
        fcols = [0, 128, 256, 384, 512, 640, 768, 896, 1536, 1664, 1792, 1920, 2048, 2176, 2304, 2432]
        fscale = [0.125] * 4 + [1.0] * 4 + [0.125] * 4 + [1.0] * 4
        tcols = [1024, 2560]
    else:
        fcols = [128 * n for n in range(16)]
        fscale = [1.0] * 16
        tcols = []
    P = Prog(nc)
    with ExitStack() as es:
        wt = P.sb(es, "A_w", [128, 8, NW], BF16)
        gt = P.sb(es, "A_g", [128, D], F32)
        C.ident = P.sb(es, "A_id", [128, 128], BF16)
        hts = [P.sb(es, "A_h%d" % j, [128, D], F32) for j in range(8)]
        hns = [P.sb(es, "A_hn%d" % j, [128, D], BF16) for j in range(2)]
        junk = P.sb(es, "A_junk", [128, D], BF16)
        sss = [P.sb(es, "A_ss%d" % j, [128, 1], F32) for j in range(2)]
        rss = [P.sb(es, "A_rs%d" % j, [128, 1], F32) for j in range(2)]
        hnTs = [P.sb(es, "A_hnT%d" % j, [128, 8, 512], BF16) for j in range(2)]
        evs = [P.sb(es, "A_ev%d" % j, [128, 512], BF16) for j in range(6)]
        pTs = [P.ps(es, "A_pT%d" % j, [128, D], BF16) for j in range(2)]
        pfs = [P.ps(es, "A_pf%d" % j, [128, 512], F32) for j in range(4)]
        make_identity(P, C.ident)
        P.dma('sp', gt.t[:], C.norm_mix[li].partition_broadcast(128), [], [gt])
        load_w_bf16(P, wt, w_ap, 8)
        NB = S // 512

        def loads(b):
            for tt in range(4):
                ht = hts[(b % 2) * 4 + tt]
                r0 = (b * 4 + tt) * 128
                P.dma('sp', ht.t[:], src_ap[r0:r0 + 128, :], [], [ht])

        def stage1(b):
            hnT = hnTs[b % 2]
            for tt in range(4):
                ht = hts[(b % 2) * 4 + tt]
                j = tt % 2
                emit_norm_T(P, C, ht, gt, hns[j], pTs[j], hnT.t[:, :, tt * 128:(tt + 1) * 128], hnT,
                            sss[j], rss[j], junk, 'act' if tt % 2 == 0 else 'dve')

        loads(0)
        if NB > 1:
            loads(1)
        stage1(0)
        nev = 0
        npf = 0
        for b in range(NB):
            if b + 1 < NB:
                stage1(b + 1)
            if b + 2 < NB:
                loads(b + 2)
            hnT = hnTs[b % 2]
            for n in range(16):
                pf = pfs[npf % 4]
                npf += 1
                c0 = fcols[n]
                for k in range(8):
                    P.pe(lambda e, k=k, pf=pf, c0=c0, hnT=hnT: e.matmul(pf.t[:], wt.t[:, k, c0:c0 + 128], hnT.t[:, k, :],
                                                                       start=(k == 0), stop=(k == 7)), [wt, hnT], [pf])
                ev = evs[nev % 6]
                nev += 1
                sc = fscale[n]
                if nev % 2 == 0:
                    P.act(lambda e, ev=ev, pf=pf, sc=sc: e.activation(ev.t[:], pf.t[:], AF.Identity, scale=sc), [pf], [ev])
                else:
                    P.dve(lambda e, ev=ev, pf=pf, sc=sc: e.tensor_scalar(ev.t[:], pf.t[:], sc, None, ALU.mult), [pf], [ev])
                P.dma('sp', C.QT[n, :, b * 512:(b + 1) * 512], ev.t[:], [ev], [], final=True)
            for ti, c0 in enumerate(tcols):
                for tt in range(4):
                    pf = pfs[npf % 4]
                    npf += 1
                    for k in range(8):
                        P.pe(lambda e, k=k, pf=pf, c0=c0, hnT=hnT, tt=tt: e.matmul(
                            pf.t[:], hnT.t[:, k, tt * 128:(tt + 1) * 128], wt.t[:, k, c0:c0 + 512],
                            start=(k == 0), stop=(k == 7)), [wt, hnT], [pf])
                    ev = evs[nev % 6]
                    nev += 1
                    if nev % 2 == 0:
                        P.act(lambda e, ev=ev, pf=pf: e.copy(ev.t[:], pf.t[:]), [pf], [ev])
                    else:
                        P.dve(lambda e, ev=ev, pf=pf: e.tensor_copy(ev.t[:], pf.t[:]), [pf], [ev])
                    r0 = (b * 4 + tt) * 128
                    P.dma('sp', C.VS[r0:r0 + 128, ti * 512:(ti + 1) * 512], ev.t[:], [ev], [], final=True)
        P.emit()


def pass_X(nc, C, li):
    even = (li % 2 == 0)
    wo_ap = (C.attn_w_out if even else C.rc_w_out)[li // 2]
    P = Prog(nc)
    with ExitStack() as es:
        wo = P.sb(es, "X_wo", [128, 8, D], BF16)
        w1 = P.sb(es, "X_w1", [128, 8, 4096], BF16)
        w2 = P.sb(es, "X_w2", [128, 32, D], BF16)
        gt = P.sb(es, "X_g", [128, D], F32)
        C.ident = P.sb(es, "X_id", [128, 128], BF16)
        yTs = [P.sb(es, "X_yT%d" % j, [128, 8, 256], BF16) for j in range(2)]
        hts = [P.sb(es, "X_h%d" % j, [128, D], F32) for j in range(4)]
        hns = [P.sb(es, "X_hn%d" % j, [128, D], BF16) for j in range(2)]
        junk = P.sb(es, "X_junk", [128, D], BF16)
        sss = [P.sb(es, "X_ss%d" % j, [128, 1], F32) for j in range(2)]
        rss = [P.sb(es, "X_rs%d" % j, [128, 1], F32) for j in range(2)]
        hnTs = [P.sb(es, "X_hnT%d" % j, [128, 8, 256], BF16) for j in range(2)]
        tmps = [P.sb(es, "X_tmp%d" % j, [128, 256], F32) for j in range(3)]
        hids = [P.sb(es, "X_hid%d" % j, [128, 256], BF16) for j in range(4)]
        pTs = [P.ps(es, "X_pT%d" % j, [128, D], BF16) for j in range(1)]
        accs = [P.ps(es, "X_acc%d" % j, [128, 512], F32) for j in range(4)]
        pms = [P.ps(es, "X_pm%d" % j, [128, 512], F32) for j in range(3)]
        make_identity(P, C.ident)
        P.dma('sp', gt.t[:], C.norm_mlp[li].partition_broadcast(128), [], [gt])
        load_w_bf16(P, wo, wo_ap, 8)
        load_w_bf16(P, w1, C.w_mlp_in[li], 8)
        load_w_bf16(P, w2, C.w_mlp_out[li], 32)
        NB = S // 256

        def loads(b):
            yT = yTs[b % 2]
            P.dma('sp', yT.t[:], C.YT[:, :, b * 256:(b + 1) * 256].rearrange("k p t -> p k t"), [], [yT])
            for tt in range(2):
                ht = hts[(b % 2) * 2 + tt]
                r0 = (b * 2 + tt) * 128
                P.dma('sp', ht.t[:], (C.x if li == 0 else C.out)[r0:r0 + 128, :], [], [ht])

        loads(0)
        nt = 0
        for b in range(NB):
            if b + 1 < NB:
                loads(b + 1)
            yT = yTs[b % 2]
            hnT = hnTs[b % 2]
            for tt in range(2):
                ht = hts[(b % 2) * 2 + tt]
                for half in range(2):
                    acc = accs[tt * 2 + half]
                    for k in range(8):
                        P.pe(lambda e, k=k, acc=acc, yT=yT, tt=tt, half=half: e.matmul(
                            acc.t[:], yT.t[:, k, tt * 128:(tt + 1) * 128], wo.t[:, k, half * 512:(half + 1) * 512],
                            start=(k == 0), stop=(k == 7)), [yT, wo], [acc])
                    P.dve(lambda e, acc=acc, ht=ht, half=half: e.tensor_tensor(
                        ht.t[:, half * 512:(half + 1) * 512], ht.t[:, half * 512:(half + 1) * 512], acc.t[:], ALU.add), [acc, ht], [ht])
                emit_norm_T(P, C, ht, gt, hns[tt], pTs[0], hnT.t[:, :, tt * 128:(tt + 1) * 128], hnT,
                            sss[tt], rss[tt], junk, 'act')

            def mlp_in(f, hnT=hnT):
                pm = pms[f % 3]
                for k in range(8):
                    P.pe(lambda e, k=k, pm=pm, f=f, hnT=hnT: e.matmul(pm.t[:, 0:256], w1.t[:, k, f * 128:(f + 1) * 128], hnT.t[:, k, :],
                                                            start=(k == 0), stop=(k == 7)), [w1, hnT], [pm])
                tmp = tmps[f % 3]
                hid = hids[f % 4]
                P.act(lambda e, tmp=tmp, pm=pm: e.activation(tmp.t[:], pm.t[:, 0:256], AF.Relu), [pm], [tmp])
                P.pool(lambda e, tmp=tmp, hid=hid: e.tensor_tensor(hid.t[:], tmp.t[:], tmp.t[:], ALU.mult), [tmp], [hid])

            def mlp_out(f):
                hid = hids[f % 4]
                for tt in range(2):
                    for half in range(2):
                        acc = accs[tt * 2 + half]
                        P.pe(lambda e, acc=acc, hid=hid, tt=tt, half=half, f=f: e.matmul(
                            acc.t[:], hid.t[:, tt * 128:(tt + 1) * 128], w2.t[:, f, half * 512:(half + 1) * 512],
                            start=(f == 0), stop=(f == 31)), [hid, w2], [acc])

            for f in range(32):
                mlp_in(f)
                if f >= 2:
                    mlp_out(f - 2)
            mlp_out(30)
            mlp_out(31)
            for tt in range(2):
                ht = hts[(b % 2) * 2 + tt]
                for half in range(2):
                    acc = accs[tt * 2 + half]
                    P.dve(lambda e, acc=acc, ht=ht, half=half: e.tensor_tensor(
                        ht.t[:, half * 512:(half + 1) * 512], ht.t[:, half * 512:(half + 1) * 512], acc.t[:], ALU.add), [acc, ht], [ht])
                r0 = (b * 2 + tt) * 128
                P.dma('sp', C.out[r0:r0 + 128, :], ht.t[:], [ht], [], final=True)
        P.emit()


def pass_Y(nc, C, li, last):
    P = Prog(nc)
    with ExitStack() as es:
        wg = P.sb(es, "Y_wg", [128, 8, D], BF16)
        wp = P.sb(es, "Y_wp", [128, 2, D], BF16)
        gt = P.sb(es, "Y_g", [128, D], F32)
        gf = P.sb(es, "Y_gf", [128, D], F32)
        C.ident = P.sb(es, "Y_id", [128, 128], BF16)
        hts = [P.sb(es, "Y_h%d" % j, [128, D], F32) for j in range(3)]
        pts = [P.sb(es, "Y_p%d" % j, [128, 256], F32) for j in range(3)]
        pbs = [P.sb(es, "Y_pb%d" % j, [128, 256], BF16) for j in range(2)]
        ppTs = [P.sb(es, "Y_ppT%d" % j, [128, 2, 128], BF16) for j in range(2)]
        hns = [P.sb(es, "Y_hn%d" % j, [128, D], BF16) for j in range(2)]
        junk = P.sb(es, "Y_junk", [128, D], BF16)
        sss = [P.sb(es, "Y_ss%d" % j, [128, 1], F32) for j in range(2)]
        rss = [P.sb(es, "Y_rs%d" % j, [128, 1], F32) for j in range(2)]
        hnTs = [P.sb(es, "Y_hnT%d" % j, [128, 8, 128], BF16) for j in range(2)]
        sgs = [P.sb(es, "Y_sg%d" % j, [128, 512], F32) for j in range(2)]
        ots = [P.sb(es, "Y_o%d" % j, [128, D], F32) for j in range(2)]
        pTs = [P.ps(es, "Y_pT%d" % j, [128, D], BF16) for j in range(2)]
        pg = [P.ps(es, "Y_pg%d" % j, [128, 512], F32) for j in range(2)]
        pp = [P.ps(es, "Y_pp%d" % j, [128, 512], F32) for j in range(2)]
        ppT = P.ps(es, "Y_ppTp", [128, 1024], BF16)
        make_identity(P, C.ident)
        P.dma('sp', gt.t[:], C.norm_ple[li].partition_broadcast(128), [], [gt])
        if last:
            P.dma('sp', gf.t[:], C.norm_final.partition_broadcast(128), [], [gf])
        load_w_bf16(P, wg, C.w_ple_gate[li], 8)
        load_w_bf16(P, wp, C.w_ple_proj[li], 2)

        def loads(t):
            ht = hts[t % 3]
            P.dma('sp', ht.t[:], C.out[t * 128:(t + 1) * 128, :], [], [ht])
            pt = pts[t % 3]
            P.dma('sp', pt.t[:], C.p[li, t * 128:(t + 1) * 128, :], [], [pt])

        def stage1(t):
            ht = hts[t % 3]
            pt = pts[t % 3]
            j = t % 2
            hnT = hnTs[j]
            emit_norm_T(P, C, ht, gt, hns[j], pTs[j], hnT.t[:, :, :], hnT, sss[j], rss[j], junk, 'act')
            pb = pbs[j]
            P.pool(lambda e, pb=pb, pt=pt: e.tensor_copy(pb.t[:], pt.t[:]), [pt], [pb])
            for k in range(2):
                P.pe(lambda e, k=k, pb=pb: e.transpose(ppT.t[:, k * 128:(k + 1) * 128], pb.t[:, k * 128:(k + 1) * 128], C.ident.t[:]),
                     [pb, C.ident], [ppT])
            pT2 = ppTs[j]
            P.dve(lambda e, pT2=pT2: e.tensor_copy(pT2.t[:, :, :], ppT.t[:, 0:256].rearrange("p (k t) -> p k t", k=2)), [ppT], [pT2])

        loads(0)
        loads(1)
        stage1(0)
        for t in range(NT):
            if t + 1 < NT:
                stage1(t + 1)
            ht = hts[t % 3]
            j = t % 2
            hnT = hnTs[j]
            pT2 = ppTs[j]
            for half in range(2):
                P_g = pg[half]
                P_p = pp[half]
                for k in range(8):
                    P.pe(lambda e, k=k, P_g=P_g, hnT=hnT, half=half: e.matmul(
                        P_g.t[:], hnT.t[:, k, :], wg.t[:, k, half * 512:(half + 1) * 512], start=(k == 0), stop=(k == 7)),
                        [hnT, wg], [P_g])
                for k in range(2):
                    P.pe(lambda e, k=k, P_p=P_p, pT2=pT2, half=half: e.matmul(
                        P_p.t[:], pT2.t[:, k, :], wp.t[:, k, half * 512:(half + 1) * 512], start=(k == 0), stop=(k == 1)),
                        [pT2, wp], [P_p])
                sg = sgs[half]
                P.act(lambda e, sg=sg, P_g=P_g: e.activation(sg.t[:], P_g.t[:], AF.Sigmoid), [P_g], [sg])
                P.dve(lambda e, sg=sg, P_p=P_p: e.tensor_tensor(sg.t[:], sg.t[:], P_p.t[:], ALU.mult), [sg, P_p], [sg])
                P.dve(lambda e, sg=sg, ht=ht, half=half: e.tensor_tensor(
                    ht.t[:, half * 512:(half + 1) * 512], ht.t[:, half * 512:(half + 1) * 512], sg.t[:], ALU.add), [sg, ht], [ht])
            if t + 2 < NT:
                loads(t + 2)
            if not last:
                P.dma('sp', C.out[t * 128:(t + 1) * 128, :], ht.t[:], [ht], [], final=True)
            else:
                ot = ots[j]
                ss, rs = sss[j], rss[j]
                P.act(lambda e, ht=ht, ss=ss: e.activation(junk.t[:], ht.t[:], AF.Square, accum_out=ss.t[:]), [ht], [junk, ss])
                P.act(lambda e, ss=ss, rs=rs: e.activation(rs.t[:], ss.t[:], AF.Sqrt, bias=EPS, scale=1.0 / D), [ss], [rs])
                P.dve(lambda e, rs=rs: e.reciprocal(rs.t[:], rs.t[:]), [rs], [rs])
                P.dve(lambda e, ot=ot, ht=ht, rs=rs: e.scalar_tensor_tensor(ot.t[:], ht.t[:], rs.t[:], gf.t[:], ALU.mult, ALU.mult),
                      [ht, rs, gf], [ot])
                P.dma('sp', C.out[t * 128:(t + 1) * 128, :], ot.t[:], [ot], [], final=True)
        P.emit()


def mm(P, out, lhsT, rhs, start, stop, reads, writes):
    return P.pe(lambda e: e.matmul(out, lhsT, rhs, start=start, stop=stop), reads, writes)


def tr(P, C, out, in_, reads, writes):
    return P.pe(lambda e: e.transpose(out, in_, C.ident.t[:]), list(reads) + [C.ident], writes)


def pass_M_even(nc, C, li):
    e_i = li // 2
    import math
    lam_init = 0.8 - 0.6 * math.exp(-0.3 * li)
    NQB = S // 512
    P = Prog(nc)
    with ExitStack() as es:
        C.ident = P.sb(es, "E_id", [128, 128], BF16)
        triu = P.sb(es, "E_triu", [128, 128], BF16)
        lq = [P.sb(es, "E_lq%d" % j, [128, 64], F32) for j in range(4)]
        lam = P.sb(es, "E_lam", [128, 4], F32)
        gsub = P.sb(es, "E_gsub", [128, 128], F32)
        QTh = [P.sb(es, "E_QT%d" % j, [128, S], BF16) for j in range(2)]
        KZ = [[P.sb(es, "E_KZ%d_%d" % (j, c), [128, S], BF16) for c in range(2)] for j in range(2)]
        Vh = [P.sb(es, "E_V%d" % j, [128, NT, 129], BF16) for j in range(2)]
        pts = [P.sb(es, "E_pt%d" % j, [128, 512], BF16) for j in range(4)]
        o1s = [P.sb(es, "E_o1%d" % j, [128, 128], F32) for j in range(4)]
        os_ = [P.sb(es, "E_o%d" % j, [128, 128], F32) for j in range(2)]
        yas = [P.sb(es, "E_ya%d" % j, [128, 128], BF16) for j in range(2)]
        junk = P.sb(es, "E_junk", [128, 128], BF16)
        junkf = P.sb(es, "E_junkf", [128, 128], F32)
        rl = [P.sb(es, "E_rl%d" % j, [128, 1], F32) for j in range(4)]
        sq = [P.sb(es, "E_sq%d" % j, [128, 1], F32) for j in range(2)]
        yTb = [P.sb(es, "E_yT%d" % j, [128, 512], BF16) for j in range(2)]
        sts = [P.ps(es, "E_st%d" % j, [128, 512], F32) for j in range(3)]
        accs = [P.ps(es, "E_acc%d" % j, [128, 512], F32) for j in range(4)]
        pTr = P.ps(es, "E_pTr", [128, 1024], BF16)
        make_identity(P, C.ident)
        P.pool(lambda e: e.memset(triu.t[:], 1.0), [], [triu])
        P.pool(lambda e: e.affine_select(triu.t[:], triu.t[:], pattern=[[1, 128]], compare_op=ALU.is_ge,
                                         fill=0.0, base=0, channel_multiplier=-1), [triu], [triu])
        for j, nm in enumerate(['diff_lq1', 'diff_lk1', 'diff_lq2', 'diff_lk2']):
            P.dma('sp', lq[j].t[:], getattr(C, nm)[e_i].partition_broadcast(128), [], [lq[j]])
        P.dma('sp', gsub.t[:], C.diff_sub_gain[e_i].partition_broadcast(128), [], [gsub])
        P.dve(lambda e: e.tensor_scalar(gsub.t[:], gsub.t[:], 1.0 - lam_init, None, ALU.mult), [gsub], [gsub])
        for j in range(2):
            P.dve(lambda e, j=j: e.tensor_tensor(lq[2 * j].t[:], lq[2 * j].t[:], lq[2 * j + 1].t[:], ALU.mult),
                  [lq[2 * j], lq[2 * j + 1]], [lq[2 * j]])
            P.dve(lambda e, j=j: e.reduce_sum(lam.t[:, j:j + 1], lq[2 * j].t[:], axis=AX.X), [lq[2 * j]], [lam])
        P.act(lambda e: e.activation(lam.t[:, 0:2], lam.t[:, 0:2], AF.Exp), [lam], [lam])
        P.dve(lambda e: e.tensor_tensor(lam.t[:, 2:3], lam.t[:, 1:2], lam.t[:, 0:1], ALU.subtract), [lam], [lam])
        P.dve(lambda e: e.tensor_scalar(lam.t[:, 2:3], lam.t[:, 2:3], -lam_init, None, ALU.add), [lam], [lam])
        for j in range(2):
            P.pool(lambda e, j=j: e.memset(Vh[j].t[:, :, 128:129], 1.0), [], [Vh[j]])
            P.pool(lambda e, j=j: e.memset(KZ[j][0].t[64:128, :], 0.0), [], [KZ[j][0]])
            P.pool(lambda e, j=j: e.memset(KZ[j][1].t[0:64, :], 0.0), [], [KZ[j][1]])
        nya = 0
        steps = []
        for h in range(4):
            for qb in range(NQB):
                for c in range(2):
                    for j in range(4 * qb + 4):
                        steps.append((h, qb, c, j))
        hbufs = {}

        def head_bufs(h):
            if h not in hbufs:
                Q, K_, V = QTh[h % 2], KZ[h % 2], Vh[h % 2]
                P.dma('sp', Q.t[:], C.QT[h], [], [Q])
                P.dma('sp', K_[0].t[0:64, :], C.QT[4 + h][0:64, :], [], [K_[0]])
                P.dma('sp', K_[1].t[64:128, :], C.QT[4 + h][64:128, :], [], [K_[1]])
                vsrc = C.VS[:, h * 128:(h + 1) * 128].rearrange("(n p) e -> p n e", p=128)
                for n0 in range(0, NT, 4):
                    P.dma('sp', V.t[:, n0:n0 + 4, 0:128], vsrc[:, n0:n0 + 4, :], [], [V])
                hbufs[h] = (Q, K_, V)
            return hbufs[h]

        def emit_st(t):
            h, qb, c, j = steps[t]
            Q, K_, V = head_bufs(h)
            p0 = c * 64
            c0 = 128 * max(0, j - 4 * qb)
            st = sts[t % 3]
            mm(P, st.t[:, c0:512], K_[c].t[:, j * 128:(j + 1) * 128],
               Q.t[:, qb * 512 + c0:(qb + 1) * 512], True, True, [K_[c], Q], [st])

        LOOK = 2
        for t in range(min(LOOK, len(steps))):
            emit_st(t)
        for t, (h, qb, c, j) in enumerate(steps):
            Q, K_, V = head_bufs(h)
            yT = yTb[qb % 2]
            i0 = max(0, j - 4 * qb)
            c0 = 128 * i0
            st = sts[t % 3]
            pt = pts[t % 4]
            P.act(lambda e, pt=pt, st=st, c0=c0: e.activation(pt.t[:, c0:512], st.t[:, c0:512], AF.Exp), [st], [pt])
            if j >= 4 * qb:
                P.pool(lambda e, pt=pt, c0=c0: e.tensor_tensor(pt.t[:, c0:c0 + 128], pt.t[:, c0:c0 + 128], triu.t[:], ALU.mult),
                       [pt, triu], [pt])
            gset = ((qb * 2 + c) % 2) * 2
            for i in range(i0, 4):
                bank = accs[gset + i // 2]
                off = (i % 2) * 256
                P.pe(lambda e, bank=bank, off=off, pt=pt, i=i, V=V, j=j, qb=qb: e.matmul(
                    bank.t[:, off:off + 129], pt.t[:, 128 * i:128 * i + 128], V.t[:, j, :],
                    start=(j == 0 and i % 2 == 0), stop=(j == 4 * qb + i), skip_group_check=True), [pt, V], [bank])
            if t + LOOK < len(steps):
                emit_st(t + LOOK)
            if qb == NQB // 2 and c == 0 and j == 0 and h + 1 < 4:
                head_bufs(h + 1)
            if j != 4 * qb + 3:
                continue
            for i in range(4):
                acc = accs[gset + i // 2]
                off = (i % 2) * 256
                r = rl[i]
                P.dve(lambda e, r=r, acc=acc, off=off: e.reciprocal(r.t[:], acc.t[:, off + 128:off + 129]), [acc], [r])
                if c == 0:
                    o1 = o1s[i]
                    P.dve(lambda e, o1=o1, acc=acc, r=r, off=off: e.tensor_scalar(o1.t[:], acc.t[:, off:off + 128], r.t[:], None, ALU.mult),
                          [acc, r], [o1])
                else:
                    o1 = o1s[i]
                    o = os_[nya % 2]
                    ya = yas[nya % 2]
                    s_ = sq[nya % 2]
                    nya += 1
                    P.dve(lambda e, r=r: e.tensor_tensor(r.t[:], r.t[:], lam.t[:, 2:3], ALU.mult), [r, lam], [r])
                    P.dve(lambda e, o=o, acc=acc, r=r, o1=o1, off=off: e.scalar_tensor_tensor(
                        o.t[:], acc.t[:, off:off + 128], r.t[:], o1.t[:], ALU.mult, ALU.add), [acc, r, o1], [o])
                    P.dve(lambda e, o=o: e.tensor_tensor(junkf.t[:], o.t[:], o.t[:], ALU.mult), [o], [junkf])
                    P.dve(lambda e, s_=s_: e.reduce_sum(s_.t[:], junkf.t[:], axis=AX.X), [junkf], [s_])
                    P.act(lambda e, s_=s_: e.activation(s_.t[:], s_.t[:], AF.Ln, bias=1e-5, scale=1.0 / 128), [s_], [s_])
                    P.act(lambda e, s_=s_: e.activation(s_.t[:], s_.t[:], AF.Exp, scale=-0.5), [s_], [s_])
                    P.dve(lambda e, ya=ya, o=o, s_=s_: e.scalar_tensor_tensor(
                        ya.t[:], o.t[:], s_.t[:], gsub.t[:], ALU.mult, ALU.mult), [o, s_, gsub], [ya])
                    tr(P, C, pTr.t[:, 0:128], ya.t[:], [ya], [pTr])
                    P.dve(lambda e, yT=yT, i=i: e.tensor_copy(yT.t[:, 128 * i:128 * i + 128], pTr.t[:, 0:128]), [pTr], [yT])
            if c == 1:
                P.dma('sp', C.YT[h, :, qb * 512:(qb + 1) * 512], yT.t[:], [yT], [], final=True)
        P.emit()
    if not getattr(C, 'skip_dil', False):
        pass_M_dil(nc, C, li)


def pass_M_dil(nc, C, li):
    DIL = (1, 4, 16)
    for ci, d in enumerate(DIL):
        L = S // d
        nb = L // 128
        P = Prog(nc)
        with ExitStack() as es:
            mask4 = P.sb(es, "D_mask", [128, 512], BF16)
            QTd = [P.sb(es, "D_QT%d" % j, [128, S], BF16) for j in range(2)]
            KZd = [[P.sb(es, "D_KZ%d_%d" % (j, hh), [128, S], BF16) for hh in range(2)] for j in range(2)]
            Vd = P.sb(es, "D_V", [128, NT, 8, 65], BF16)
            pts = [P.sb(es, "D_pt%d" % j, [128, 512], BF16) for j in range(4)]
            obs = [P.sb(es, "D_ob%d" % j, [128, 2, 65], F32) for j in range(4)]
            sts = [P.ps(es, "D_st%d" % j, [128, 512], F32) for j in range(3)]
            accs = [P.ps(es, "D_acc%d" % j, [128, 512], F32) for j in range(4)]
            for j in range(2):
                P.pool(lambda e, j=j: e.memset(KZd[j][0].t[64:128, :], 0.0), [], [KZd[j][0]])
                P.pool(lambda e, j=j: e.memset(KZd[j][1].t[0:64, :], 0.0), [], [KZd[j][1]])
            P.pool(lambda e: e.memset(mask4.t[:], 1.0), [], [mask4])
            for q4 in range(4):
                if q4 % 2 == 0:
                    P.pool(lambda e, q4=q4: e.affine_select(mask4.t[:, q4 * 128:(q4 + 1) * 128], mask4.t[:, q4 * 128:(q4 + 1) * 128],
                                                            pattern=[[1, 128]], compare_op=ALU.is_ge, fill=0.0, base=0,
                                                            channel_multiplier=-1), [mask4], [mask4])
                else:
                    P.pool(lambda e, q4=q4: e.affine_select(mask4.t[:, q4 * 128:(q4 + 1) * 128], mask4.t[:, q4 * 128:(q4 + 1) * 128],
                                                            pattern=[[-1, 128]], compare_op=ALU.is_ge, fill=0.0, base=0,
                                                            channel_multiplier=1), [mask4], [mask4])
            P.pool(lambda e: e.memset(Vd.t[:, :, :, 64:65], 1.0), [], [Vd])
            Vst = P.sb(es, "D_Vst", [128, NT, 512], BF16)
            vsrc = C.VS[:, 512:1024].rearrange("(n m r) c -> r m n c", m=128, r=d)
            for r in range(d):
                for n0 in range(0, nb, 4):
                    n1 = min(nb, n0 + 4)
                    P.dma('sp', Vst.t[:, r * nb + n0:r * nb + n1, :], vsrc[r][:, n0:n1, :], [], [Vst])
            for q4 in range(4):
                n0, n1 = q4 * NT // 4, (q4 + 1) * NT // 4
                eng = P.pool if q4 % 2 == 0 else P.act
                if q4 % 2 == 0:
                    P.pool(lambda e, n0=n0, n1=n1: e.tensor_copy(Vd.t[:, n0:n1, :, 0:64], Vst.t[:, n0:n1, :].rearrange("p n (h e) -> p n h e", h=8)),
                           [Vst], [Vd])
                else:
                    P.act(lambda e, n0=n0, n1=n1: e.copy(Vd.t[:, n0:n1, :, 0:64], Vst.t[:, n0:n1, :].rearrange("p n (h e) -> p n h e", h=8)),
                          [Vst], [Vd])
            dosrc = C.DO[ci].rearrange("(n m r) c -> r n m c", m=128, r=d)
            nob = 0
            steps = [(hp, r, n) for hp in range(4) for r in range(d) for n in range(nb)]
            hb = {}

            def hp_bufs(hp):
                if hp not in hb:
                    Q, K_ = QTd[hp % 2], KZd[hp % 2]
                    P.dma('sp', Q.t[:], C.QT[8 + hp], [], [Q])
                    P.dma('sp', K_[0].t[0:64, :], C.QT[12 + hp][0:64, :], [], [K_[0]])
                    P.dma('sp', K_[1].t[64:128, :], C.QT[12 + hp][64:128, :], [], [K_[1]])
                    hb[hp] = (Q, K_)
                return hb[hp]

            def emit_qk(t):
                hp, r, n = steps[t]
                Q, K_ = hp_bufs(hp)
                nq = 2 if n + 1 < nb else 1
                k0 = n * 128 * d + r
                st = sts[t % 3]
                for hh in range(2):
                    mm(P, st.t[:, hh * 256:hh * 256 + nq * 128], K_[hh].t[:, k0:k0 + 127 * d + 1:d],
                       Q.t[:, k0:k0 + (nq * 128 - 1) * d + 1:d], True, True, [K_[hh], Q], [st])

            LOOKD = 2
            for t in range(min(LOOKD, len(steps))):
                emit_qk(t)
            for t, (hp, r, n) in enumerate(steps):
                kb = r * nb + n
                nq = 2 if n + 1 < nb else 1
                pt = pts[t % 4]
                st = sts[t % 3]
                if nq == 2:
                    P.act(lambda e, pt=pt, st=st: e.activation(pt.t[:], st.t[:], AF.Exp), [st], [pt])
                else:
                    for hh in range(2):
                        P.act(lambda e, pt=pt, st=st, hh=hh: e.activation(pt.t[:, hh * 256:hh * 256 + 128],
                                                                      st.t[:, hh * 256:hh * 256 + 128], AF.Exp), [st], [pt])
                if nq == 2:
                    P.dve(lambda e, pt=pt: e.tensor_tensor(pt.t[:], pt.t[:], mask4.t[:], ALU.mult), [pt, mask4], [pt])
                else:
                    for hh in range(2):
                        P.dve(lambda e, pt=pt, hh=hh: e.tensor_tensor(pt.t[:, hh * 256:hh * 256 + 128], pt.t[:, hh * 256:hh * 256 + 128],
                                                                     mask4.t[:, 0:128], ALU.mult), [pt, mask4], [pt])
                if t + LOOKD < len(steps):
                    emit_qk(t + LOOKD)
                for hh in range(2):
                    hidx = 2 * hp + hh
                    for qq in range(nq):
                        acc = accs[hh * 2 + (n + qq) % 2]
                        mm(P, acc.t[:, 0:65], pt.t[:, hh * 256 + qq * 128:hh * 256 + qq * 128 + 128], Vd.t[:, kb, hidx, :],
                           qq == 1 or n == 0, qq == 0, [pt, Vd], [acc])
                ob = obs[nob % 4]
                nob += 1
                for hh in range(2):
                    acc = accs[hh * 2 + n % 2]
                    if hh == 0:
                        P.act(lambda e, ob=ob, acc=acc, hh=hh: e.copy(ob.t[:, hh, :], acc.t[:, 0:65]), [acc], [ob])
                    else:
                        P.dve(lambda e, ob=ob, acc=acc, hh=hh: e.tensor_copy(ob.t[:, hh, :], acc.t[:, 0:65]), [acc], [ob])
                P.dma('sp', dosrc[r, n][:, hp * 130:(hp + 1) * 130], ob.t[:, :, :].rearrange("p a b -> p (a b)"), [ob], [], final=True)
            P.emit()
    P = Prog(nc)
    with ExitStack() as es:
        C.ident = P.sb(es, "G_id", [128, 128], BF16)
        dts = [P.sb(es, "G_d%d" % j, [128, 3, 520], F32) for j in range(2)]
        rd = [P.sb(es, "G_rd%d" % j, [128, 8], F32) for j in range(2)]
        yb = [P.sb(es, "G_yb%d" % j, [128, 512], BF16) for j in range(2)]
        yT = [P.sb(es, "G_yT%d" % j, [128, 4, 128], BF16) for j in range(2)]
        pTr = [P.ps(es, "G_pT%d" % j, [128, 1024], BF16) for j in range(2)]
        make_identity(P, C.ident)
        for t in range(NT):
            dt_ = dts[t % 2]
            j = t % 2
            P.dma('sp', dt_.t[:], C.DO[:, t * 128:(t + 1) * 128, :].rearrange("c p f -> p c f"), [], [dt_])
            P.dve(lambda e, dt_=dt_: e.tensor_tensor(dt_.t[:, 0, :], dt_.t[:, 0, :], dt_.t[:, 1, :], ALU.add), [dt_], [dt_])
            P.dve(lambda e, dt_=dt_: e.tensor_tensor(dt_.t[:, 0, :], dt_.t[:, 0, :], dt_.t[:, 2, :], ALU.add), [dt_], [dt_])
            v3 = dt_.t[:, 0, :].rearrange("p (h e) -> p h e", h=8)
            P.dve(lambda e, v3=v3, j=j: e.reciprocal(rd[j].t[:, :], v3[:, :, 64]), [dt_], [rd[j]])
            P.dve(lambda e, v3=v3, j=j: e.tensor_tensor(yb[j].t[:, :].rearrange("p (h e) -> p h e", h=8), v3[:, :, 0:64],
                                                       rd[j].t[:, :].unsqueeze(2).to_broadcast([128, 8, 64]), ALU.mult),
                  [dt_, rd[j]], [yb[j]])
            for k in range(4):
                tr(P, C, pTr[j].t[:, k * 128:(k + 1) * 128], yb[j].t[:, k * 128:(k + 1) * 128], [yb[j]], [pTr[j]])
            P.act(lambda e, j=j: e.copy(yT[j].t[:, :, :], pTr[j].t[:, 0:512].rearrange("p (k t) -> p k t", k=4)), [pTr[j]], [yT[j]])
            P.dma('sp', C.YT[4:8, :, t * 128:(t + 1) * 128].rearrange("k p t -> p k t"), yT[j].t[:, :, :], [yT[j]], [], final=True)
        P.emit()


def make_identity_f32(P, ident):
    t = ident.t
    P.pool(lambda e: e.memset(t[:], 1.0), [], [ident])
    P.pool(lambda e: e.affine_select(t[:], t[:], pattern=[[-1, 128]], compare_op=ALU.is_equal,
                                    fill=0.0, base=0, channel_multiplier=1), [ident], [ident])


def pass_M_odd(nc, C, li):
    import math
    o = li // 2
    TB = 512
    NB = S // TB
    TWO_PI = 2.0 * math.pi
    P = Prog(nc)
    with ExitStack() as es:
        identF = P.sb(es, "O_idF", [128, 128], F32)
        C.ident = P.sb(es, "O_id", [128, 128], BF16)
        stg = [P.sb(es, "O_stg%d" % j, [32, 128], F32) for j in range(2)]
        lr = P.sb(es, "O_lr", [128, 16], F32)
        lim = P.sb(es, "O_li", [128, 16], F32)
        dtv = P.sb(es, "O_dt", [128, 16], F32)
        ld2 = P.sb(es, "O_ld2", [16, 2], F32)
        th = P.sb(es, "O_th", [128, 16], F32)
        mag = P.sb(es, "O_mag", [128, 16], F32)
        ph = [P.sb(es, "O_ph%d" % j, [128, 16], F32) for j in range(2)]
        tq = P.sb(es, "O_tq", [128, 16], F32)
        cs = P.sb(es, "O_cs", [128, 16], F32)
        sn = P.sb(es, "O_sn", [128, 16], F32)
        w_ = [P.sb(es, "O_w%d" % j, [128, 16], F32) for j in range(6)]
        cre = P.sb(es, "O_cre", [128, 16], F32)
        cim = P.sb(es, "O_cim", [128, 16], F32)
        dT = P.sb(es, "O_d", [128, 4], F32)
        cwv = P.sb(es, "O_cw", [128, 12], F32)
        CTt = P.sb(es, "O_CT", [128, 16, TB], F32)
        STt = P.sb(es, "O_ST", [128, 16, TB], F32)
        tA = P.sb(es, "O_tA", [128, 8, 256], F32)
        tB = P.sb(es, "O_tB", [128, 8, 256], F32)
        Bst = [P.sb(es, "O_Bst%d" % j, [128, 16, 16], F32) for j in range(2)]
        bb = [P.sb(es, "O_bb%d" % j, [128, 16, 16], F32) for j in range(2)]
        bt_ = [P.sb(es, "O_bt%d" % j, [128, 16, 16], F32) for j in range(2)]
        CC = [P.sb(es, "O_CC%d" % j, [128, 4, 64], F32) for j in range(2)]
        maskB = [P.sb(es, "O_mB%d" % j, [128, 128], F32) for j in range(4)]
        maskC = [P.sb(es, "O_mC%d" % j, [128, 128], F32) for j in range(4)]
        BF = [P.sb(es, "O_BF%d" % j, [128, 128], BF16) for j in range(2)]
        BT = [P.sb(es, "O_BT%d" % j, [128, 16, 128], BF16) for j in range(2)]
        CM = [P.sb(es, "O_CM%d" % j, [128, 16, 128], BF16) for j in range(2)]
        wglu = P.sb(es, "O_wglu", [128, 4, 512], BF16)
        xprev = [P.sb(es, "O_xp%d" % j, [128, 16], F32) for j in range(2)]
        ins_ = [[P.sb(es, "O_in%d_%d" % (q, j), [128, 4, TB], BF16) for j in range(1)] for q in range(4)]
        T = [P.sb(es, "O_T%d" % j, [128, TB], F32) for j in range(4)]
        btr = P.sb(es, "O_btr", [128, TB], F32)
        bti = P.sb(es, "O_bti", [128, TB], F32)
        zr = P.sb(es, "O_zr", [128, TB], F32)
        zi = P.sb(es, "O_zi", [128, TB], F32)
        xrf = [P.sb(es, "O_xrf%d" % j, [128, TB], F32) for j in range(2)]
        xif = [P.sb(es, "O_xif%d" % j, [128, TB], F32) for j in range(2)]
        xrb = [P.sb(es, "O_xrb%d" % j, [128, TB], BF16) for j in range(2)]
        xib = [P.sb(es, "O_xib%d" % j, [128, TB], BF16) for j in range(2)]
        yf = P.sb(es, "O_yf", [128, TB], F32)
        g1 = P.sb(es, "O_g1", [128, TB], F32)
        g2 = P.sb(es, "O_g2", [128, TB], F32)
        ygf = P.sb(es, "O_ygf", [128, 4, TB], F32)
        ygb = P.sb(es, "O_ygb", [128, 4, TB], BF16)
        sg2 = [P.sb(es, "O_sg%d" % j, [128, TB], F32) for j in range(2)]
        ycT = [P.sb(es, "O_yc%d" % j, [128, TB], BF16) for j in range(2)]
        zb = [P.sb(es, "O_zb%d" % j, [128, TB + 2], F32) for j in range(4)]
        yv = [P.sb(es, "O_yv%d" % j, [128, TB], F32) for j in range(1)]
        ydT = [P.sb(es, "O_yd%d" % j, [128, TB], BF16) for j in range(2)]
        ytmps = [P.sb(es, "O_ytmp%d" % j, [128, TB], F32) for j in range(2)]
        pmisc = P.ps(es, "O_pm", [128, 512], F32)
        pTr = P.ps(es, "O_pT", [128, 1024], BF16)
        pb = [P.ps(es, "O_pb%d" % j, [128, 512], F32) for j in range(4)]
        py = [P.ps(es, "O_py%d" % j, [128, 512], F32) for j in range(2)]

        make_identity_f32(P, identF)
        make_identity(P, C.ident)
        load_w_bf16(P, wglu, C.ssm_w_glu[o], 4)
        nstg = [0]

        def load_T(dst_ap, dst_buf, src_ap, n):
            st_ = stg[nstg[0] % 2]
            nstg[0] += 1
            P.dma('sp', st_.t[0:n, :], src_ap, [], [st_])
            P.pe(lambda e: e.matmul(pmisc.t[:, 0:n], st_.t[0:n, :], identF.t[0:n, 0:n], start=True, stop=True),
                 [st_, identF], [pmisc])
            P.dve(lambda e: e.tensor_copy(dst_ap, pmisc.t[:, 0:n]), [pmisc], [dst_buf])

        load_T(lr.t[:, :], lr, C.ssm_lambda_re[o].rearrange("(j a) p -> j (a p)", a=2), 16)
        load_T(lim.t[:, :], lim, C.ssm_lambda_im[o].rearrange("(j a) p -> j (a p)", a=2), 16)
        load_T(dT.t[:, :], dT, C.ssm_d[o].rearrange("(t a) c -> t (a c)", a=8), 4)
        load_T(cwv.t[:, :], cwv, C.conv_w[o].rearrange("k (t c) -> (k t) c", c=128), 12)
        P.dma('sp', ld2.t[:, :], C.ssm_log_dt[o].rearrange("(j a) -> j a", a=2), [], [ld2])
        st_ = stg[nstg[0] % 2]
        nstg[0] += 1
        P.dve(lambda e: e.tensor_copy(st_.t[0:16, :].rearrange("j (a p) -> j a p", a=2),
                                      ld2.t[:, :].unsqueeze(2).to_broadcast([16, 2, 64])), [ld2], [st_])
        P.pe(lambda e: e.matmul(pmisc.t[:, 0:16], st_.t[0:16, :], identF.t[0:16, 0:16], start=True, stop=True), [st_, identF], [pmisc])
        P.act(lambda e: e.activation(dtv.t[:], pmisc.t[:, 0:16], AF.Exp), [pmisc], [dtv])
        P.dve(lambda e: e.tensor_tensor(th.t[:], lim.t[:], dtv.t[:], ALU.mult), [lim, dtv], [th])
        P.dve(lambda e: e.tensor_tensor(mag.t[:], lr.t[:], dtv.t[:], ALU.mult), [lr, dtv], [mag])
        P.act(lambda e: e.activation(mag.t[:], mag.t[:], AF.Exp), [mag], [mag])

        def range_reduce(dst, shift):
            P.dve(lambda e: e.tensor_scalar(dst.t[:], th.t[:], shift, None, ALU.add), [th], [dst])
            for m_ in range(1, 21):
                thr = (2 * m_ - 1) * math.pi - shift
                P.dve(lambda e, thr=thr: e.tensor_scalar(tq.t[:], th.t[:], thr, -TWO_PI, ALU.is_ge, ALU.mult), [th], [tq])
                P.dve(lambda e: e.tensor_tensor(dst.t[:], dst.t[:], tq.t[:], ALU.add), [dst, tq], [dst])

        range_reduce(ph[0], 0.0)
        range_reduce(ph[1], math.pi / 2)
        P.act(lambda e: e.activation(sn.t[:], ph[0].t[:], AF.Sin), [ph[0]], [sn])
        P.act(lambda e: e.activation(cs.t[:], ph[1].t[:], AF.Sin), [ph[1]], [cs])
        abr, abi, den, nr, u1, u2 = w_
        TT = lambda out, a, b, op: P.dve(lambda e: e.tensor_tensor(out.t[:], a.t[:], b.t[:], op), [a, b], [out])
        TT(abr, mag, cs, ALU.mult)
        TT(abi, mag, sn, ALU.mult)
        TT(den, lr, lr, ALU.mult)
        TT(u1, lim, lim, ALU.mult)
        TT(den, den, u1, ALU.add)
        P.dve(lambda e: e.reciprocal(den.t[:], den.t[:]), [den], [den])
        P.dve(lambda e: e.tensor_scalar(nr.t[:], abr.t[:], -1.0, None, ALU.add), [abr], [nr])
        TT(u1, nr, lr, ALU.mult)
        TT(u2, abi, lim, ALU.mult)
        TT(u1, u1, u2, ALU.add)
        TT(cre, u1, den, ALU.mult)
        TT(u1, abi, lr, ALU.mult)
        TT(u2, nr, lim, ALU.mult)
        TT(u1, u1, u2, ALU.subtract)
        TT(cim, u1, den, ALU.mult)
        P.dve(lambda e: e.tensor_copy(CTt.t[:, :, 0], cs.t[:, :]), [cs], [CTt])
        P.dve(lambda e: e.tensor_copy(STt.t[:, :, 0], sn.t[:, :]), [sn], [STt])
        w = 1
        while w < TB:
            for jh in range(2):
                js = slice(jh * 8, jh * 8 + 8)
                cwb = CTt.t[:, js, w - 1].unsqueeze(2).to_broadcast([128, 8, w])
                swb = STt.t[:, js, w - 1].unsqueeze(2).to_broadcast([128, 8, w])
                c0, s0 = CTt.t[:, js, 0:w], STt.t[:, js, 0:w]
                c1, s1 = CTt.t[:, js, w:2 * w], STt.t[:, js, w:2 * w]
                a_, b_ = tA.t[:, :, 0:w], tB.t[:, :, 0:w]
                P.dve(lambda e, a_=a_, c0=c0, cwb=cwb: e.tensor_tensor(a_, c0, cwb, ALU.mult), [CTt], [tA])
                P.pool(lambda e, b_=b_, s0=s0, swb=swb: e.tensor_tensor(b_, s0, swb, ALU.mult), [STt, CTt], [tB])
                P.dve(lambda e, a_=a_, b_=b_, c1=c1: e.tensor_tensor(c1, a_, b_, ALU.subtract), [tA, tB], [CTt])
                P.dve(lambda e, a_=a_, s0=s0, cwb=cwb: e.tensor_tensor(a_, s0, cwb, ALU.mult), [STt, CTt], [tA])
                P.pool(lambda e, b_=b_, c0=c0, swb=swb: e.tensor_tensor(b_, c0, swb, ALU.mult), [CTt, STt], [tB])
                P.dve(lambda e, a_=a_, b_=b_, s1=s1: e.tensor_tensor(s1, a_, b_, ALU.add), [tA, tB], [STt])
            w *= 2
        for m_ in range(4):
            P.pool(lambda e, m_=m_: e.memset(maskB[m_].t[:], 1.0), [], [maskB[m_]])
            P.pool(lambda e, m_=m_: e.memset(maskC[m_].t[:], 1.0), [], [maskC[m_]])
            for a in range(2):
                blk = 16 * (2 * m_ + a)
                tB_ = maskB[m_].t[a * 64:(a + 1) * 64, :]
                P.pool(lambda e, tB_=tB_, blk=blk: e.affine_select(tB_, tB_, pattern=[[1, 128]], compare_op=ALU.is_ge, fill=0.0,
                                                                base=-blk, channel_multiplier=0), [maskB[m_]], [maskB[m_]])
                P.pool(lambda e, tB_=tB_, blk=blk: e.affine_select(tB_, tB_, pattern=[[-1, 128]], compare_op=ALU.is_ge, fill=0.0,
                                                                base=blk + 15, channel_multiplier=0), [maskB[m_]], [maskB[m_]])
                tC_ = maskC[m_].t[:, a * 64:(a + 1) * 64]
                P.pool(lambda e, tC_=tC_, blk=blk: e.affine_select(tC_, tC_, pattern=[[0, 64]], compare_op=ALU.is_ge, fill=0.0,
                                                                base=-blk, channel_multiplier=1), [maskC[m_]], [maskC[m_]])
                P.pool(lambda e, tC_=tC_, blk=blk: e.affine_select(tC_, tC_, pattern=[[0, 64]], compare_op=ALU.is_ge, fill=0.0,
                                                                base=blk + 15, channel_multiplier=-1), [maskC[m_]], [maskC[m_]])
        for ri, nm in enumerate(['ssm_b_re', 'ssm_b_im']):
            src = getattr(C, nm)[o].rearrange("(j a) p c -> (a p) j c", a=2)
            for j0 in range(0, 16, 4):
                P.dma('sp', Bst[ri].t[:, j0:j0 + 4, :], src[:, j0:j0 + 4, :], [], [Bst[ri]])
        creb = cre.t[:, :].unsqueeze(2).to_broadcast([128, 16, 16])
        cimb = cim.t[:, :].unsqueeze(2).to_broadcast([128, 16, 16])
        P.dve(lambda e: e.tensor_tensor(bb[0].t[:], Bst[0].t[:], creb, ALU.mult), [Bst[0], cre], [bb[0]])
        P.dve(lambda e: e.tensor_tensor(bt_[0].t[:], Bst[1].t[:], cimb, ALU.mult), [Bst[1], cim], [bt_[0]])
        P.dve(lambda e: e.tensor_tensor(bb[0].t[:], bb[0].t[:], bt_[0].t[:], ALU.subtract), [bb[0], bt_[0]], [bb[0]])
        P.dve(lambda e: e.tensor_tensor(bb[1].t[:], Bst[1].t[:], creb, ALU.mult), [Bst[1], cre], [bb[1]])
        P.dve(lambda e: e.tensor_tensor(bt_[1].t[:], Bst[0].t[:], cimb, ALU.mult), [Bst[0], cim], [bt_[1]])
        P.dve(lambda e: e.tensor_tensor(bb[1].t[:], bb[1].t[:], bt_[1].t[:], ALU.add), [bb[1], bt_[1]], [bb[1]])
        for ri, nm in enumerate(['ssm_c_re', 'ssm_c_im']):
            src = getattr(C, nm)[o].rearrange("(t g) c p -> (g c) t p", g=8)
            P.dma('sp', CC[ri].t[:, :, :], src, [], [CC[ri]])
        nbf = 0
        for j in range(16):
            for ri in range(2):
                bf = BF[nbf % 2]
                nbf += 1
                P.dve(lambda e, bf=bf, j=j, ri=ri: e.tensor_tensor(
                    bf.t[:, :].rearrange("q (k c) -> q k c", k=8), maskB[j % 4].t[:, :].rearrange("q (k c) -> q k c", k=8),
                    bb[ri].t[:, j, :].unsqueeze(1).to_broadcast([128, 8, 16]), ALU.mult), [maskB[j % 4], bb[ri]], [bf])
                tr(P, C, pTr.t[:, 0:128], bf.t[:], [bf], [pTr])
                P.act(lambda e, j=j, ri=ri: e.copy(BT[ri].t[:, j, :], pTr.t[:, 0:128]), [pTr], [BT[ri]])
            for ri in range(2):
                bf = BF[nbf % 2]
                nbf += 1
                sgn = 1.0 if ri == 0 else -1.0
                P.dve(lambda e, bf=bf, j=j, ri=ri, sgn=sgn: e.scalar_tensor_tensor(
                    bf.t[:, :].rearrange("q (a p) -> q a p", a=2), maskC[j % 4].t[:, :].rearrange("q (a p) -> q a p", a=2), sgn,
                    CC[ri].t[:, j // 4, :].unsqueeze(1).to_broadcast([128, 2, 64]), ALU.mult, ALU.mult), [maskC[j % 4], CC[ri]], [bf])
                tr(P, C, pTr.t[:, 0:128], bf.t[:], [bf], [pTr])
                P.act(lambda e, j=j, ri=ri: e.copy(CM[ri].t[:, j, :], pTr.t[:, 0:128]), [pTr], [CM[ri]])
        for q in range(2):
            P.pool(lambda e, q=q: e.memset(xprev[q].t[:], 0.0), [], [xprev[q]])
        for ct in range(4):
            P.pool(lambda e, ct=ct: e.memset(zb[ct].t[:, TB:TB + 2], 0.0), [], [zb[ct]])

        def loads(b):
            for q in range(4):
                t_ = ins_[q][0]
                P.dma('sp', t_.t[:, :, :], C.QT[4 * q:4 * q + 4, :, b * TB:(b + 1) * TB].rearrange("k p t -> p k t"), [], [t_])

        nx = 0
        nyc = 0
        for b in range(NB):
            loads(b)
            uT, gbT, gcT, xtT = [ins_[q][0] for q in range(4)]
            for ct in range(4):
                z = zb[ct]
                y_ = yv[0]
                yd = ydT[ct % 2]
                P.act(lambda e, z=z: e.copy(z.t[:, 0:2], z.t[:, TB:TB + 2]), [z], [z])
                P.dve(lambda e, z=z, ct=ct, gcT=gcT, xtT=xtT: e.tensor_tensor(z.t[:, 2:TB + 2], gcT.t[:, ct, :], xtT.t[:, ct, :], ALU.mult),
                      [gcT, xtT, z], [z])
                P.act(lambda e, z=z, y_=y_, ct=ct: e.activation(y_.t[:], z.t[:, 2:TB + 2], AF.Copy, scale=cwv.t[:, ct:ct + 1]), [z, cwv], [y_])
                for kk in (1, 2):
                    yt_ = ytmps[kk - 1]
                    P.act(lambda e, z=z, ct=ct, kk=kk, yt_=yt_: e.activation(yt_.t[:], z.t[:, 2 - kk:TB + 2 - kk], AF.Copy,
                                                                            scale=cwv.t[:, 4 * kk + ct:4 * kk + ct + 1]), [z, cwv], [yt_])
                    P.dve(lambda e, y_=y_, yt_=yt_: e.tensor_tensor(y_.t[:], y_.t[:], yt_.t[:], ALU.add), [y_, yt_], [y_])
                P.dve(lambda e, yd=yd, y_=y_, ct=ct, gbT=gbT: e.tensor_tensor(yd.t[:], gbT.t[:, ct, :], y_.t[:], ALU.mult), [gbT, y_], [yd])
                P.dma('sp', C.YT[4 + ct, :, b * TB:(b + 1) * TB], yd.t[:], [yd], [], final=True)
            for j in range(16):
                ct = j // 4
                pbr, pbi = pb[(j % 2) * 2], pb[(j % 2) * 2 + 1]
                pyc = py[ct % 2]
                mm(P, pbr.t[:], BT[0].t[:, j, :], uT.t[:, ct, :], True, True, [BT[0], uT], [pbr])
                mm(P, pbi.t[:], BT[1].t[:, j, :], uT.t[:, ct, :], True, True, [BT[1], uT], [pbi])
                cj, sj = CTt.t[:, j, :], STt.t[:, j, :]
                V = lambda out, a_ap, b_ap, op, rd, wr: P.dve(lambda e: e.tensor_tensor(out, a_ap, b_ap, op), rd, wr)
                V(T[0].t[:], cj, pbr.t[:], ALU.mult, [CTt, pbr], [T[0]])
                V(T[1].t[:], sj, pbi.t[:], ALU.mult, [STt, pbi], [T[1]])
                V(btr.t[:], T[0].t[:], T[1].t[:], ALU.add, [T[0], T[1]], [btr])
                V(T[2].t[:], cj, pbi.t[:], ALU.mult, [CTt, pbi], [T[2]])
                V(T[3].t[:], sj, pbr.t[:], ALU.mult, [STt, pbr], [T[3]])
                V(bti.t[:], T[2].t[:], T[3].t[:], ALU.subtract, [T[2], T[3]], [bti])
                rdec = mag.t[:, j:j + 1].to_broadcast([128, TB])
                P.dve(lambda e, rdec=rdec, j=j: e.tensor_tensor_scan(zr.t[:], rdec, btr.t[:], xprev[0].t[:, j:j + 1], ALU.mult, ALU.add),
                      [mag, btr, xprev[0]], [zr])
                P.dve(lambda e, rdec=rdec, j=j: e.tensor_tensor_scan(zi.t[:], rdec, bti.t[:], xprev[1].t[:, j:j + 1], ALU.mult, ALU.add),
                      [mag, bti, xprev[1]], [zi])
                xr_, xi_ = xrf[nx % 2], xif[nx % 2]
                xrb_, xib_ = xrb[nx % 2], xib[nx % 2]
                nx += 1
                V(T[0].t[:], cj, zr.t[:], ALU.mult, [CTt, zr], [T[0]])
                V(T[1].t[:], sj, zi.t[:], ALU.mult, [STt, zi], [T[1]])
                V(xr_.t[:], T[0].t[:], T[1].t[:], ALU.subtract, [T[0], T[1]], [xr_])
                V(T[2].t[:], sj, zr.t[:], ALU.mult, [STt, zr], [T[2]])
                V(T[3].t[:], cj, zi.t[:], ALU.mult, [CTt, zi], [T[3]])
                V(xi_.t[:], T[2].t[:], T[3].t[:], ALU.add, [T[2], T[3]], [xi_])
                P.act(lambda e, xr_=xr_, j=j: e.copy(xprev[0].t[:, j:j + 1], xr_.t[:, TB - 1:TB]), [xr_], [xprev[0]])
                P.act(lambda e, xi_=xi_, j=j: e.copy(xprev[1].t[:, j:j + 1], xi_.t[:, TB - 1:TB]), [xi_], [xprev[1]])
                P.act(lambda e, xr_=xr_, xrb_=xrb_: e.copy(xrb_.t[:], xr_.t[:]), [xr_], [xrb_])
                P.act(lambda e, xi_=xi_, xib_=xib_: e.copy(xib_.t[:], xi_.t[:]), [xi_], [xib_])
                mm(P, pyc.t[:], CM[0].t[:, j, :], xrb_.t[:], j % 4 == 0, False, [CM[0], xrb_], [pyc])
                mm(P, pyc.t[:], CM[1].t[:, j, :], xib_.t[:], False, j % 4 == 3, [CM[1], xib_], [pyc])
                if j % 4 == 3:
                    P.dve(lambda e, ct=ct, pyc=pyc, uT=uT: e.scalar_tensor_tensor(yf.t[:], uT.t[:, ct, :], dT.t[:, ct:ct + 1], pyc.t[:], ALU.mult, ALU.add),
                          [uT, dT, pyc], [yf])
                    P.act(lambda e: e.activation(g1.t[:], yf.t[:], AF.Square), [yf], [g1])
                    P.dve(lambda e: e.tensor_scalar(g1.t[:], g1.t[:], 0.044715, 1.0, ALU.mult, ALU.add), [g1], [g1])
                    P.dve(lambda e: e.tensor_tensor(g1.t[:], g1.t[:], yf.t[:], ALU.mult), [g1, yf], [g1])
                    P.act(lambda e: e.activation(g2.t[:], g1.t[:], AF.Sigmoid, scale=1.5957691216057308), [g1], [g2])
                    P.dve(lambda e, ct=ct: e.tensor_tensor(ygf.t[:, ct, :], yf.t[:], g2.t[:], ALU.mult), [yf, g2], [ygf])
                    P.act(lambda e, ct=ct: e.copy(ygb.t[:, ct, :], ygf.t[:, ct, :]), [ygf], [ygb])
            for co in range(4):
                pg_ = pb[co]
                for k in range(4):
                    mm(P, pg_.t[:], wglu.t[:, k, co * 128:(co + 1) * 128], ygb.t[:, k, :], k == 0, k == 3, [wglu, ygb], [pg_])
                sg_ = sg2[co % 2]
                yc = ycT[nyc % 2]
                nyc += 1
                P.act(lambda e, sg_=sg_, pg_=pg_: e.activation(sg_.t[:], pg_.t[:], AF.Sigmoid), [pg_], [sg_])
                P.dve(lambda e, yc=yc, sg_=sg_, co=co: e.tensor_tensor(yc.t[:], ygf.t[:, co, :], sg_.t[:], ALU.mult), [ygf, sg_], [yc])
                P.dma('sp', C.YT[co, :, b * TB:(b + 1) * TB], yc.t[:], [yc], [], final=True)
        P.emit()


SKIP_DIL = False
WEIGHTS = [
    ('norm_mix', [4, 1024]), ('norm_mlp', [4, 1024]), ('norm_ple', [4, 1024]),
    ('w_mlp_in', [4, 1024, 4096]), ('w_mlp_out', [4, 4096, 1024]), ('w_ple_proj', [4, 256, 1024]),
    ('w_ple_gate', [4, 1024, 1024]), ('attn_w_in', [2, 1024, 3072]), ('attn_w_out', [2, 1024, 1024]),
    ('diff_lq1', [2, 64]), ('diff_lk1', [2, 64]), ('diff_lq2', [2, 64]), ('diff_lk2', [2, 64]),
    ('diff_sub_gain', [2, 128]), ('rc_w_in', [2, 1024, 2048]), ('rc_w_out', [2, 1024, 1024]),
    ('ssm_lambda_re', [2, 32, 64]), ('ssm_lambda_im', [2, 32, 64]), ('ssm_log_dt', [2, 32]),
    ('ssm_b_re', [2, 32, 64, 16]), ('ssm_b_im', [2, 32, 64, 16]), ('ssm_c_re', [2, 32, 16, 64]),
    ('ssm_c_im', [2, 32, 16, 64]), ('ssm_d', [2, 32, 16]), ('ssm_w_glu', [2, 512, 512]),
    ('conv_w', [2, 3, 512]), ('norm_final', [1024]),
]


def build_nc(passes=None, ext=()):
    nc = bass.Bass("TRN2", target_bir_lowering=False)
    C = Ctx()
    C.skip_dil = SKIP_DIL
    C.x = nc.dram_tensor("x", [S, D], F32, kind="ExternalInput").ap()
    C.p = nc.dram_tensor("p", [4, S, 256], F32, kind="ExternalInput").ap()
    for name, shp in WEIGHTS:
        setattr(C, name, nc.dram_tensor(name, shp, F32, kind="ExternalInput").ap())
    C.out = nc.dram_tensor("out", [S, D], F32, kind="ExternalOutput").ap()

    def scratch(name, shp, dt):
        kind = "Internal"
        for nm, k in ext:
            if nm == name:
                kind = k
        return nc.dram_tensor(name, shp, dt, kind=kind).ap()

    C.QT = scratch("QT", [16, 128, S], BF16)
    C.VS = scratch("VS", [S, D], BF16)
    C.YT = scratch("YT", [8, 128, S], BF16)
    C.DO = scratch("DO", [3, S, 8 * 65], F32)
    allp = []
    for li in range(4):
        allp += ["A%d" % li, "M%d" % li, "X%d" % li, "Y%d" % li]
    if passes is None:
        passes = allp
    for pn in passes:
        li = int(pn[1])
        if pn[0] == 'A':
            pass_A(nc, C, li, C.x if li == 0 else C.out)
        elif pn[0] == 'M':
            if li % 2 == 0:
                pass_M_even(nc, C, li)
            else:
                pass_M_odd(nc, C, li)
        elif pn[0] == 'X':
            pass_X(nc, C, li)
        elif pn[0] == 'Y':
            pass_Y(nc, C, li, last=(li == 3))
        elif pn[0] == 'C':
            pass_copy(nc, C)
    return nc


def pass_copy(nc, C):
    P = Prog(nc)
    with ExitStack() as es:
        hts = [P.sb(es, "C_h%d" % j, [128, D], F32) for j in range(4)]
        for t in range(NT):
            ht = hts[t % 4]
            P.dma('sp', ht.t[:], C.x[t * 128:(t + 1) * 128, :], [], [ht])
            P.dma('sp', C.out[t * 128:(t + 1) * 128, :], ht.t[:], [ht], [], final=True)
        P.emit()


def kernel(**inputs):
    nc = build_nc()
    shared = {k: np.ascontiguousarray(np.asarray(inputs[k], dtype=np.float32)) for k, _ in WEIGHTS}
    x = np.asarray(inputs['x'], dtype=np.float32)
    p = np.asarray(inputs['p'], dtype=np.float32)
    in_maps = []
    for b in range(8):
        m = dict(shared)
        m['x'] = np.ascontiguousarray(x[b])
        m['p'] = np.ascontiguousarray(p[:, b])
        in_maps.append(m)
    res = run_bass_kernel_spmd(nc, in_maps, core_ids=list(range(8)))
    return np.stack([np.asarray(r['out'], dtype=np.float32) for r in res.results], axis=0)
```

```python
import numpy as np
from contextlib import ExitStack
import concourse.bass as bass
import concourse.mybir as mybir
from concourse.bass_utils import run_bass_kernel_spmd

F32 = mybir.dt.float32
BF16 = mybir.dt.bfloat16
ALU = mybir.AluOpType
AF = mybir.ActivationFunctionType
AX = mybir.AxisListType

COMPUTE = ('pe', 'act', 'dve', 'pool')


class Buf:
    def __init__(self, name, t=None):
        self.name = name
        self.t = t
        self.last_w = None
        self.readers = []
        self.sem = None
        self.dma_cnt = 0


class Op:
    __slots__ = ('eng', 'fn', 'is_dma', 'deps', 'flag', 'cnt', 'dbuf', 'dval', 'idx', 'sem')

    def __init__(self, eng, fn, is_dma):
        self.eng = eng
        self.fn = fn
        self.is_dma = is_dma
        self.deps = []
        self.flag = False
        self.cnt = 0
        self.dbuf = None
        self.dval = 0


class Prog:
    def __init__(self, nc):
        self.nc = nc
        self.ops = []
        self.finals = []
        self.n_sb = 0

    uid = [0]

    def sb(self, es, name, shape, dtype):
        Prog.uid[0] += 1
        name = "%s_%d" % (name, Prog.uid[0])
        t = es.enter_context(self.nc.sbuf_tensor(name, list(shape), dtype))
        return Buf(name, t)

    def ps(self, es, name, shape, dtype):
        Prog.uid[0] += 1
        name = "%s_%d" % (name, Prog.uid[0])
        t = es.enter_context(self.nc.psum_tensor(name, list(shape), dtype))
        return Buf(name, t)

    def view(self, name, t):
        return Buf(name, t)

    def _add(self, op, reads, writes):
        deps = {}
        for b in reads:
            if b.last_w is not None:
                deps[id(b.last_w)] = (b.last_w, 'raw')
        for b in writes:
            if b.last_w is not None:
                deps[id(b.last_w)] = (b.last_w, 'waw')
            for r in b.readers:
                if id(r) not in deps:
                    deps[id(r)] = (r, 'war')
        for b in reads:
            b.readers.append(op)
        for b in writes:
            b.last_w = op
            b.readers = []
        for p, kind in deps.values():
            if p is op:
                continue
            if not p.is_dma:
                if p.eng == op.eng and not op.is_dma:
                    if p.eng == 'pe':
                        continue
                p.flag = True
            op.deps.append(p)
        op.idx = len(self.ops)
        self.ops.append(op)
        return op

    def op(self, eng, fn, reads, writes):
        return self._add(Op(eng, fn, False), reads, writes)

    def pe(self, fn, reads, writes):
        return self.op('pe', fn, reads, writes)

    def act(self, fn, reads, writes):
        return self.op('act', fn, reads, writes)

    def dve(self, fn, reads, writes):
        return self.op('dve', fn, reads, writes)

    def pool(self, fn, reads, writes):
        return self.op('pool', fn, reads, writes)

    def dma(self, q, out, in_, reads, writes, sbuf=None, final=False, **kw):
        if sbuf is None:
            cands = [b for b in list(reads) + list(writes) if getattr(b, 'is_sb', True) and b.t is not None]
            sbuf = cands[0]
        op = Op(q, (lambda e: e.dma_start(out=out, in_=in_, **kw)), True)
        op.dbuf = sbuf
        sbuf.dma_cnt += 16
        op.dval = sbuf.dma_cnt
        self._add(op, reads, writes)
        if final:
            self.finals.append(op)
        return op

    def emit(self):
        nc = self.nc
        with ExitStack() as es:
            EPOCH = 1000
            esem = {}
            allsems = []
            for op in self.ops:
                if op.is_dma and op.dbuf.sem is None:
                    Prog.uid[0] += 1
                    op.dbuf.sem = nc.alloc_semaphore(name='d_%d_%s' % (Prog.uid[0], op.dbuf.name))
                    allsems.append(op.dbuf.sem)
            cnt = {e: 0 for e in COMPUTE}
            for op in self.ops:
                if not op.is_dma and op.flag:
                    ep = cnt[op.eng] // EPOCH
                    if (op.eng, ep) not in esem:
                        Prog.uid[0] += 1
                        esem[(op.eng, ep)] = nc.alloc_semaphore(name='s_%s_%d_%d' % (op.eng, ep, Prog.uid[0]))
                        allsems.append(esem[(op.eng, ep)])
                    op.sem = esem[(op.eng, ep)]
                    op.cnt = cnt[op.eng] % EPOCH + 1
                    cnt[op.eng] += 1
            per_eng = {e: [] for e in ('pe', 'act', 'dve', 'pool', 'sp')}
            for op in self.ops:
                per_eng[op.eng].append(op)
            finals = self.finals
            self.n_inst = {e: len(v) for e, v in per_eng.items()}

            def run(eng_name, e):
                waited = {}
                for op in per_eng[eng_name]:
                    need = {}
                    for p in op.deps:
                        if p.is_dma:
                            s, v = p.dbuf.sem, p.dval
                        else:
                            s, v = p.sem, p.cnt
                        k = id(s)
                        if k not in need or need[k][1] < v:
                            need[k] = (s, v)
                    for k, (s, v) in need.items():
                        if waited.get(k, 0) >= v:
                            continue
                        waited[k] = v
                        e.wait_ge(s, v)
                    ins = op.fn(e)
                    if op.is_dma:
                        ins.then_inc(op.dbuf.sem, 16)
                    elif op.flag:
                        ins.then_inc(op.sem, 1)
                if eng_name == 'sp':
                    need = {}
                    for p in finals:
                        s, v = p.dbuf.sem, p.dval
                        if id(s) not in need or need[id(s)][1] < v:
                            need[id(s)] = (s, v)
                    for s, v in need.values():
                        e.wait_ge(s, v)

            with nc.Block() as block:
                @block.sync
                def _(e):
                    run('sp', e)

                @block.tensor
                def _(e):
                    run('pe', e)

                @block.scalar
                def _(e):
                    run('act', e)

                @block.vector
                def _(e):
                    run('dve', e)

                @block.gpsimd
                def _(e):
                    run('pool', e)
            nc.all_engine_barrier()
            nc.clear_and_free_semaphores(allsems)
            nc.all_engine_barrier()


def bcast_rows(ap, n):
    return ap.partition_broadcast(n)


def make_identity(P, ident):
    t = ident.t
    P.pool(lambda e: e.memset(t[:], 1.0), [], [ident])
    P.pool(lambda e: e.affine_select(t[:], t[:], pattern=[[-1, 128]], compare_op=ALU.is_equal,
                                    fill=0.0, base=0, channel_multiplier=1), [ident], [ident])


S = 4096
D = 1024
NT = S // 128
EPS = 1e-6


class Ctx:
    pass


def rot(lst, i):
    return lst[i % len(lst)]


def load_w_bf16(P, wt, w_ap, nk, q='pool'):
    if not hasattr(P, 'wtok'):
        P.wtok = [Buf('wtok%d' % j) for j in range(3)]
        P.nw = 0
    for k in range(nk):
        tok = P.wtok[P.nw % 3]
        P.nw += 1
        P.dma(q, wt.t[:, k, :], w_ap[k * 128:(k + 1) * 128, :], reads=[], writes=[wt, tok], sbuf=wt)


def emit_norm_T(P, C, ht, gt, hn, pT, hnT_dst, hnT_buf, ss, rs, junk, evac_eng):
    P.act(lambda e: e.activation(junk.t[:], ht.t[:], AF.Square, accum_out=ss.t[:]), [ht], [junk, ss])
    P.act(lambda e: e.activation(rs.t[:], ss.t[:], AF.Sqrt, bias=EPS, scale=1.0 / D), [ss], [rs])
    P.dve(lambda e: e.reciprocal(rs.t[:], rs.t[:]), [rs], [rs])
    P.dve(lambda e: e.scalar_tensor_tensor(hn.t[:], ht.t[:], rs.t[:], gt.t[:], ALU.mult, ALU.mult), [ht, rs, gt], [hn])
    for k in range(8):
        P.pe(lambda e, k=k: e.transpose(pT.t[:, k * 128:(k + 1) * 128], hn.t[:, k * 128:(k + 1) * 128], C.ident.t[:]),
             [hn, C.ident], [pT])
    src = pT.t[:, :].rearrange("p (k t) -> p k t", k=8)
    if evac_eng == 'act':
        P.act(lambda e: e.copy(hnT_dst, src), [pT], [hnT_buf])
    else:
        P.dve(lambda e: e.tensor_copy(hnT_dst, src), [pT], [hnT_buf])


def pass_A(nc, C, li, src_ap):
    even = (li % 2 == 0)
    NW = 3072 if even else 2048
    w_ap = (C.attn_w_in if even else C.rc_w_in)[li // 2]
    if even:
        fcols = [0, 128, 256, 384, 512, 640, 768, 896, 1536, 1664, 1792, 1920, 2048, 2176, 2304, 2432]
        fscale = [0.125] * 4 + [1.0] * 4 + [0.125] * 4 + [1.0] * 4
        tcols = [1024, 2560]
    else:
        fcols = [128 * n for n in range(16)]
        fscale = [1.0] * 16
        tcols = []
    P = Prog(nc)
    with ExitStack() as es:
        wt = P.sb(es, "A_w", [128, 8, NW], BF16)
        gt = P.sb(es, "A_g", [128, D], F32)
        C.ident = P.sb(es, "A_id", [128, 128], BF16)
        hts = [P.sb(es, "A_h%d" % j, [128, D], F32) for j in range(8)]
        hns = [P.sb(es, "A_hn%d" % j, [128, D], BF16) for j in range(2)]
        junk = P.sb(es, "A_junk", [128, D], BF16)
        sss = [P.sb(es, "A_ss%d" % j, [128, 1], F32) for j in range(2)]
        rss = [P.sb(es, "A_rs%d" % j, [128, 1], F32) for j in range(2)]
        hnTs = [P.sb(es, "A_hnT%d" % j, [128, 8, 512], BF16) for j in range(2)]
        evs = [P.sb(es, "A_ev%d" % j, [128, 512], BF16) for j in range(6)]
        pTs = [P.ps(es, "A_pT%d" % j, [128, D], BF16) for j in range(2)]
        pfs = [P.ps(es, "A_pf%d" % j, [128, 512], F32) for j in range(4)]
        make_identity(P, C.ident)
        P.dma('sp', gt.t[:], C.norm_mix[li].partition_broadcast(128), [], [gt])
        load_w_bf16(P, wt, w_ap, 8)
        NB = S // 512

        def loads(b):
            for tt in range(4):
                ht = hts[(b % 2) * 4 + tt]
                r0 = (b * 4 + tt) * 128
                P.dma('sp', ht.t[:], src_ap[r0:r0 + 128, :], [], [ht])

        def stage1(b):
            hnT = hnTs[b % 2]
            for tt in range(4):
                ht = hts[(b % 2) * 4 + tt]
                j = tt % 2
                emit_norm_T(P, C, ht, gt, hns[j], pTs[j], hnT.t[:, :, tt * 128:(tt + 1) * 128], hnT,
                            sss[j], rss[j], junk, 'act' if tt % 2 == 0 else 'dve')

        loads(0)
        if NB > 1:
            loads(1)
        stage1(0)
        nev = 0
        npf = 0
        for b in range(NB):
            if b + 1 < NB:
                stage1(b + 1)
            if b + 2 < NB:
                loads(b + 2)
            hnT = hnTs[b % 2]
            for n in range(16):
                pf = pfs[npf % 4]
                npf += 1
                c0 = fcols[n]
                for k in range(8):
                    P.pe(lambda e, k=k, pf=pf, c0=c0, hnT=hnT: e.matmul(pf.t[:], wt.t[:, k, c0:c0 + 128], hnT.t[:, k, :],
                                                                       start=(k == 0), stop=(k == 7)), [wt, hnT], [pf])
                ev = evs[nev % 6]
                nev += 1
                sc = fscale[n]
                if nev % 2 == 0:
                    P.act(lambda e, ev=ev, pf=pf, sc=sc: e.activation(ev.t[:], pf.t[:], AF.Identity, scale=sc), [pf], [ev])
                else:
                    P.dve(lambda e, ev=ev, pf=pf, sc=sc: e.tensor_scalar(ev.t[:], pf.t[:], sc, None, ALU.mult), [pf], [ev])
                P.dma('sp', C.QT[n, :, b * 512:(b + 1) * 512], ev.t[:], [ev], [], final=True)
            for ti, c0 in enumerate(tcols):
                for tt in range(4):
                    pf = pfs[npf % 4]
                    npf += 1
                    for k in range(8):
                        P.pe(lambda e, k=k, pf=pf, c0=c0, hnT=hnT, tt=tt: e.matmul(
                            pf.t[:], hnT.t[:, k, tt * 128:(tt + 1) * 128], wt.t[:, k, c0:c0 + 512],
                            start=(k == 0), stop=(k == 7)), [wt, hnT], [pf])
                    ev = evs[nev % 6]
                    nev += 1
                    if nev % 2 == 0:
                        P.act(lambda e, ev=ev, pf=pf: e.copy(ev.t[:], pf.t[:]), [pf], [ev])
                    else:
                        P.dve(lambda e, ev=ev, pf=pf: e.tensor_copy(ev.t[:], pf.t[:]), [pf], [ev])
                    r0 = (b * 4 + tt) * 128
                    P.dma('sp', C.VS[r0:r0 + 128, ti * 512:(ti + 1) * 512], ev.t[:], [ev], [], final=True)
        P.emit()


def pass_X(nc, C, li):
    even = (li % 2 == 0)
    wo_ap = (C.attn_w_out if even else C.rc_w_out)[li // 2]
    P = Prog(nc)
    with ExitStack() as es:
        wo = P.sb(es, "X_wo", [128, 8, D], BF16)
        w1 = P.sb(es, "X_w1", [128, 8, 4096], BF16)
        w2 = P.sb(es, "X_w2", [128, 32, D], BF16)
        gt = P.sb(es, "X_g", [128, D], F32)
        C.ident = P.sb(es, "X_id", [128, 128], BF16)
        yTs = [P.sb(es, "X_yT%d" % j, [128, 8, 256], BF16) for j in range(2)]
        hts = [P.sb(es, "X_h%d" % j, [128, D], F32) for j in range(4)]
        hns = [P.sb(es, "X_hn%d" % j, [128, D], BF16) for j in range(2)]
        junk = P.sb(es, "X_junk", [128, D], BF16)
        sss = [P.sb(es, "X_ss%d" % j, [128, 1], F32) for j in range(2)]
        rss = [P.sb(es, "X_rs%d" % j, [128, 1], F32) for j in range(2)]
        hnTs = [P.sb(es, "X_hnT%d" % j, [128, 8, 256], BF16) for j in range(2)]
        tmps = [P.sb(es, "X_tmp%d" % j, [128, 256], F32) for j in range(3)]
        hids = [P.sb(es, "X_hid%d" % j, [128, 256], BF16) for j in range(4)]
        pTs = [P.ps(es, "X_pT%d" % j, [128, D], BF16) for j in range(1)]
        accs = [P.ps(es, "X_acc%d" % j, [128, 512], F32) for j in range(4)]
        pms = [P.ps(es, "X_pm%d" % j, [128, 512], F32) for j in range(3)]
        make_identity(P, C.ident)
        P.dma('sp', gt.t[:], C.norm_mlp[li].partition_broadcast(128), [], [gt])
        load_w_bf16(P, wo, wo_ap, 8)
        load_w_bf16(P, w1, C.w_mlp_in[li], 8)
        load_w_bf16(P, w2, C.w_mlp_out[li], 32)
        NB = S // 256

        def loads(b):
            yT = yTs[b % 2]
            P.dma('sp', yT.t[:], C.YT[:, :, b * 256:(b + 1) * 256].rearrange("k p t -> p k t"), [], [yT])
            for tt in range(2):
                ht = hts[(b % 2) * 2 + tt]
                r0 = (b * 2 + tt) * 128
                P.dma('sp', ht.t[:], (C.x if li == 0 else C.out)[r0:r0 + 128, :], [], [ht])

        loads(0)
        nt = 0
        for b in range(NB):
            if b + 1 < NB:
                loads(b + 1)
            yT = yTs[b % 2]
            hnT = hnTs[b % 2]
            for tt in range(2):
                ht = hts[(b % 2) * 2 + tt]
                for half in range(2):
                    acc = accs[tt * 2 + half]
                    for k in range(8):
                        P.pe(lambda e, k=k, acc=acc, yT=yT, tt=tt, half=half: e.matmul(
                            acc.t[:], yT.t[:, k, tt * 128:(tt + 1) * 128], wo.t[:, k, half * 512:(half + 1) * 512],
                            start=(k == 0), stop=(k == 7)), [yT, wo], [acc])
                    P.dve(lambda e, acc=acc, ht=ht, half=half: e.tensor_tensor(
                        ht.t[:, half * 512:(half + 1) * 512], ht.t[:, half * 512:(half + 1) * 512], acc.t[:], ALU.add), [acc, ht], [ht])
                emit_norm_T(P, C, ht, gt, hns[tt], pTs[0], hnT.t[:, :, tt * 128:(tt + 1) * 128], hnT,
                            sss[tt], rss[tt], junk, 'act')

            def mlp_in(f, hnT=hnT):
                pm = pms[f % 3]
                for k in range(8):
                    P.pe(lambda e, k=k, pm=pm, f=f, hnT=hnT: e.matmul(pm.t[:, 0:256], w1.t[:, k, f * 128:(f + 1) * 128], hnT.t[:, k, :],
                                                            start=(k == 0), stop=(k == 7)), [w1, hnT], [pm])
                tmp = tmps[f % 3]
                hid = hids[f % 4]
                P.act(lambda e, tmp=tmp, pm=pm: e.activation(tmp.t[:], pm.t[:, 0:256], AF.Relu), [pm], [tmp])
                P.pool(lambda e, tmp=tmp, hid=hid: e.tensor_tensor(hid.t[:], tmp.t[:], tmp.t[:], ALU.mult), [tmp], [hid])

            def mlp_out(f):
                hid = hids[f % 4]
                for tt in range(2):
                    for half in range(2):
                        acc = accs[tt * 2 + half]
                        P.pe(lambda e, acc=acc, hid=hid, tt=tt, half=half, f=f: e.matmul(
                            acc.t[:], hid.t[:, tt * 128:(tt + 1) * 128], w2.t[:, f, half * 512:(half + 1) * 512],
                            start=(f == 0), stop=(f == 31)), [hid, w2], [acc])

            for f in range(32):
                mlp_in(f)
                if f >= 2:
                    mlp_out(f - 2)
            mlp_out(30)
            mlp_out(31)
            for tt in range(2):
                ht = hts[(b % 2) * 2 + tt]
                for half in range(2):
                    acc = accs[tt * 2 + half]
                    P.dve(lambda e, acc=acc, ht=ht, half=half: e.tensor_tensor(
                        ht.t[:, half * 512:(half + 1) * 512], ht.t[:, half * 512:(half + 1) * 512], acc.t[:], ALU.add), [acc, ht], [ht])
                r0 = (b * 2 + tt) * 128
                P.dma('sp', C.out[r0:r0 + 128, :], ht.t[:], [ht], [], final=True)
        P.emit()


def pass_Y(nc, C, li, last):
    P = Prog(nc)
    with ExitStack() as es:
        wg = P.sb(es, "Y_wg", [128, 8, D], BF16)
        wp = P.sb(es, "Y_wp", [128, 2, D], BF16)
        gt = P.sb(es, "Y_g", [128, D], F32)
        gf = P.sb(es, "Y_gf", [128, D], F32)
        C.ident = P.sb(es, "Y_id", [128, 128], BF16)
        hts = [P.sb(es, "Y_h%d" % j, [128, D], F32) for j in range(3)]
        pts = [P.sb(es, "Y_p%d" % j, [128, 256], F32) for j in range(3)]
        pbs = [P.sb(es, "Y_pb%d" % j, [128, 256], BF16) for j in range(2)]
        ppTs = [P.sb(es, "Y_ppT%d" % j, [128, 2, 128], BF16) for j in range(2)]
        hns = [P.sb(es, "Y_hn%d" % j, [128, D], BF16) for j in range(2)]
        junk = P.sb(es, "Y_junk", [128, D], BF16)
        sss = [P.sb(es, "Y_ss%d" % j, [128, 1], F32) for j in range(2)]
        rss = [P.sb(es, "Y_rs%d" % j, [128, 1], F32) for j in range(2)]
        hnTs = [P.sb(es, "Y_hnT%d" % j, [128, 8, 128], BF16) for j in range(2)]
        sgs = [P.sb(es, "Y_sg%d" % j, [128, 512], F32) for j in range(2)]
        ots = [P.sb(es, "Y_o%d" % j, [128, D], F32) for j in range(2)]
        pTs = [P.ps(es, "Y_pT%d" % j, [128, D], BF16) for j in range(2)]
        pg = [P.ps(es, "Y_pg%d" % j, [128, 512], F32) for j in range(2)]
        pp = [P.ps(es, "Y_pp%d" % j, [128, 512], F32) for j in range(2)]
        ppT = P.ps(es, "Y_ppTp", [128, 1024], BF16)
        make_identity(P, C.ident)
        P.dma('sp', gt.t[:], C.norm_ple[li].partition_broadcast(128), [], [gt])
        if last:
            P.dma('sp', gf.t[:], C.norm_final.partition_broadcast(128), [], [gf])
        load_w_bf16(P, wg, C.w_ple_gate[li], 8)
        load_w_bf16(P, wp, C.w_ple_proj[li], 2)

        def loads(t):
            ht = hts[t % 3]
            P.dma('sp', ht.t[:], C.out[t * 128:(t + 1) * 128, :], [], [ht])
            pt = pts[t % 3]
            P.dma('sp', pt.t[:], C.p[li, t * 128:(t + 1) * 128, :], [], [pt])

        def stage1(t):
            ht = hts[t % 3]
            pt = pts[t % 3]
            j = t % 2
            hnT = hnTs[j]
            emit_norm_T(P, C, ht, gt, hns[j], pTs[j], hnT.t[:, :, :], hnT, sss[j], rss[j], junk, 'act')
            pb = pbs[j]
            P.pool(lambda e, pb=pb, pt=pt: e.tensor_copy(pb.t[:], pt.t[:]), [pt], [pb])
            for k in range(2):
                P.pe(lambda e, k=k, pb=pb: e.transpose(ppT.t[:, k * 128:(k + 1) * 128], pb.t[:, k * 128:(k + 1) * 128], C.ident.t[:]),
                     [pb, C.ident], [ppT])
            pT2 = ppTs[j]
            P.dve(lambda e, pT2=pT2: e.tensor_copy(pT2.t[:, :, :], ppT.t[:, 0:256].rearrange("p (k t) -> p k t", k=2)), [ppT], [pT2])

        loads(0)
        loads(1)
        stage1(0)
        for t in range(NT):
            if t + 1 < NT:
                stage1(t + 1)
            ht = hts[t % 3]
            j = t % 2
            hnT = hnTs[j]
            pT2 = ppTs[j]
            for half in range(2):
                P_g = pg[half]
                P_p = pp[half]
                for k in range(8):
                    P.pe(lambda e, k=k, P_g=P_g, hnT=hnT, half=half: e.matmul(
                        P_g.t[:], hnT.t[:, k, :], wg.t[:, k, half * 512:(half + 1) * 512], start=(k == 0), stop=(k == 7)),
                        [hnT, wg], [P_g])
                for k in range(2):
                    P.pe(lambda e, k=k, P_p=P_p, pT2=pT2, half=half: e.matmul(
                        P_p.t[:], pT2.t[:, k, :], wp.t[:, k, half * 512:(half + 1) * 512], start=(k == 0), stop=(k == 1)),
                        [pT2, wp], [P_p])
                sg = sgs[half]
                P.act(lambda e, sg=sg, P_g=P_g: e.activation(sg.t[:], P_g.t[:], AF.Sigmoid), [P_g], [sg])
                P.dve(lambda e, sg=sg, P_p=P_p: e.tensor_tensor(sg.t[:], sg.t[:], P_p.t[:], ALU.mult), [sg, P_p], [sg])
                P.dve(lambda e, sg=sg, ht=ht, half=half: e.tensor_tensor(
                    ht.t[:, half * 512:(half + 1) * 512], ht.t[:, half * 512:(half + 1) * 512], sg.t[:], ALU.add), [sg, ht], [ht])
            if t + 2 < NT:
                loads(t + 2)
            if not last:
                P.dma('sp', C.out[t * 128:(t + 1) * 128, :], ht.t[:], [ht], [], final=True)
            else:
                ot = ots[j]
                ss, rs = sss[j], rss[j]
                P.act(lambda e, ht=ht, ss=ss: e.activation(junk.t[:], ht.t[:], AF.Square, accum_out=ss.t[:]), [ht], [junk, ss])
                P.act(lambda e, ss=ss, rs=rs: e.activation(rs.t[:], ss.t[:], AF.Sqrt, bias=EPS, scale=1.0 / D), [ss], [rs])
                P.dve(lambda e, rs=rs: e.reciprocal(rs.t[:], rs.t[:]), [rs], [rs])
                P.dve(lambda e, ot=ot, ht=ht, rs=rs: e.scalar_tensor_tensor(ot.t[:], ht.t[:], rs.t[:], gf.t[:], ALU.mult, ALU.mult),
                      [ht, rs, gf], [ot])
                P.dma('sp', C.out[t * 128:(t + 1) * 128, :], ot.t[:], [ot], [], final=True)
        P.emit()


def mm(P, out, lhsT, rhs, start, stop, reads, writes):
    return P.pe(lambda e: e.matmul(out, lhsT, rhs, start=start, stop=stop), reads, writes)


def tr(P, C, out, in_, reads, writes):
    return P.pe(lambda e: e.transpose(out, in_, C.ident.t[:]), list(reads) + [C.ident], writes)


def pass_M_even(nc, C, li):
    e_i = li // 2
    import math
    lam_init = 0.8 - 0.6 * math.exp(-0.3 * li)
    NQB = S // 512
    P = Prog(nc)
    with ExitStack() as es:
        C.ident = P.sb(es, "E_id", [128, 128], BF16)
        triu = P.sb(es, "E_triu", [128, 128], BF16)
        lq = [P.sb(es, "E_lq%d" % j, [128, 64], F32) for j in range(4)]
        lam = P.sb(es, "E_lam", [128, 4], F32)
        gsub = P.sb(es, "E_gsub", [128, 128], F32)
        QTh = [P.sb(es, "E_QT%d" % j, [128, S], BF16) for j in range(2)]
        KZ = [[P.sb(es, "E_KZ%d_%d" % (j, c), [128, S], BF16) for c in range(2)] for j in range(2)]
        Vh = [P.sb(es, "E_V%d" % j, [128, NT, 129], BF16) for j in range(2)]
        pts = [P.sb(es, "E_pt%d" % j, [128, 512], BF16) for j in range(4)]
        o1s = [P.sb(es, "E_o1%d" % j, [128, 128], F32) for j in range(4)]
        os_ = [P.sb(es, "E_o%d" % j, [128, 128], F32) for j in range(2)]
        yas = [P.sb(es, "E_ya%d" % j, [128, 128], BF16) for j in range(2)]
        junk = P.sb(es, "E_junk", [128, 128], BF16)
        junkf = P.sb(es, "E_junkf", [128, 128], F32)
        rl = [P.sb(es, "E_rl%d" % j, [128, 1], F32) for j in range(4)]
        sq = [P.sb(es, "E_sq%d" % j, [128, 1], F32) for j in range(2)]
        yTb = [P.sb(es, "E_yT%d" % j, [128, 512], BF16) for j in range(2)]
        sts = [P.ps(es, "E_st%d" % j, [128, 512], F32) for j in range(3)]
        accs = [P.ps(es, "E_acc%d" % j, [128, 512], F32) for j in range(4)]
        pTr = P.ps(es, "E_pTr", [128, 1024], BF16)
        make_identity(P, C.ident)
        P.pool(lambda e: e.memset(triu.t[:], 1.0), [], [triu])
        P.pool(lambda e: e.affine_select(triu.t[:], triu.t[:], pattern=[[1, 128]], compare_op=ALU.is_ge,
                                         fill=0.0, base=0, channel_multiplier=-1), [triu], [triu])
        for j, nm in enumerate(['diff_lq1', 'diff_lk1', 'diff_lq2', 'diff_lk2']):
            P.dma('sp', lq[j].t[:], getattr(C, nm)[e_i].partition_broadcast(128), [], [lq[j]])
        P.dma('sp', gsub.t[:], C.diff_sub_gain[e_i].partition_broadcast(128), [], [gsub])
        P.dve(lambda e: e.tensor_scalar(gsub.t[:], gsub.t[:], 1.0 - lam_init, None, ALU.mult), [gsub], [gsub])
        for j in range(2):
            P.dve(lambda e, j=j: e.tensor_tensor(lq[2 * j].t[:], lq[2 * j].t[:], lq[2 * j + 1].t[:], ALU.mult),
                  [lq[2 * j], lq[2 * j + 1]], [lq[2 * j]])
            P.dve(lambda e, j=j: e.reduce_sum(lam.t[:, j:j + 1], lq[2 * j].t[:], axis=AX.X), [lq[2 * j]], [lam])
        P.act(lambda e: e.activation(lam.t[:, 0:2], lam.t[:, 0:2], AF.Exp), [lam], [lam])
        P.dve(lambda e: e.tensor_tensor(lam.t[:, 2:3], lam.t[:, 1:2], lam.t[:, 0:1], ALU.subtract), [lam], [lam])
        P.dve(lambda e: e.tensor_scalar(lam.t[:, 2:3], lam.t[:, 2:3], -lam_init, None, ALU.add), [lam], [lam])
        for j in range(2):
            P.pool(lambda e, j=j: e.memset(Vh[j].t[:, :, 128:129], 1.0), [], [Vh[j]])
            P.pool(lambda e, j=j: e.memset(KZ[j][0].t[64:128, :], 0.0), [], [KZ[j][0]])
            P.pool(lambda e, j=j: e.memset(KZ[j][1].t[0:64, :], 0.0), [], [KZ[j][1]])
        nya = 0
        steps = []
        for h in range(4):
            for qb in range(NQB):
                for c in range(2):
                    for j in range(4 * qb + 4):
                        steps.append((h, qb, c, j))
        hbufs = {}

        def head_bufs(h):
            if h not in hbufs:
                Q, K_, V = QTh[h % 2], KZ[h % 2], Vh[h % 2]
                P.dma('sp', Q.t[:], C.QT[h], [], [Q])
                P.dma('sp', K_[0].t[0:64, :], C.QT[4 + h][0:64, :], [], [K_[0]])
                P.dma('sp', K_[1].t[64:128, :], C.QT[4 + h][64:128, :], [], [K_[1]])
                vsrc = C.VS[:, h * 128:(h + 1) * 128].rearrange("(n p) e -> p n e", p=128)
                for n0 in range(0, NT, 4):
                    P.dma('sp', V.t[:, n0:n0 + 4, 0:128], vsrc[:, n0:n0 + 4, :], [], [V])
                hbufs[h] = (Q, K_, V)
            return hbufs[h]

        def emit_st(t):
            h, qb, c, j = steps[t]
            Q, K_, V = head_bufs(h)
            p0 = c * 64
            c0 = 128 * max(0, j - 4 * qb)
            st = sts[t % 3]
            mm(P, st.t[:, c0:512], K_[c].t[:, j * 128:(j + 1) * 128],
               Q.t[:, qb * 512 + c0:(qb + 1) * 512], True, True, [K_[c], Q], [st])

        LOOK = 2
        for t in range(min(LOOK, len(steps))):
            emit_st(t)
        for t, (h, qb, c, j) in enumerate(steps):
            Q, K_, V = head_bufs(h)
            yT = yTb[qb % 2]
            i0 = max(0, j - 4 * qb)
            c0 = 128 * i0
            st = sts[t % 3]
            pt = pts[t % 4]
            P.act(lambda e, pt=pt, st=st, c0=c0: e.activation(pt.t[:, c0:512], st.t[:, c0:512], AF.Exp), [st], [pt])
            if j >= 4 * qb:
                P.pool(lambda e, pt=pt, c0=c0: e.tensor_tensor(pt.t[:, c0:c0 + 128], pt.t[:, c0:c0 + 128], triu.t[:], ALU.mult),
                       [pt, triu], [pt])
            gset = ((qb * 2 + c) % 2) * 2
            for i in range(i0, 4):
                bank = accs[gset + i // 2]
                off = (i % 2) * 256
                P.pe(lambda e, bank=bank, off=off, pt=pt, i=i, V=V, j=j, qb=qb: e.matmul(
                    bank.t[:, off:off + 129], pt.t[:, 128 * i:128 * i + 128], V.t[:, j, :],
                    start=(j == 0 and i % 2 == 0), stop=(j == 4 * qb + i), skip_group_check=True), [pt, V], [bank])
            if t + LOOK < len(steps):
                emit_st(t + LOOK)
            if qb == NQB // 2 and c == 0 and j == 0 and h + 1 < 4:
                head_bufs(h + 1)
            if j != 4 * qb + 3:
                continue
            for i in range(4):
                acc = accs[gset + i // 2]
                off = (i % 2) * 256
                r = rl[i]
                P.dve(lambda e, r=r, acc=acc, off=off: e.reciprocal(r.t[:], acc.t[:, off + 128:off + 129]), [acc], [r])
                if c == 0:
                    o1 = o1s[i]
                    P.dve(lambda e, o1=o1, acc=acc, r=r, off=off: e.tensor_scalar(o1.t[:], acc.t[:, off:off + 128], r.t[:], None, ALU.mult),
                          [acc, r], [o1])
                else:
                    o1 = o1s[i]
                    o = os_[nya % 2]
                    ya = yas[nya % 2]
                    s_ = sq[nya % 2]
                    nya += 1
                    P.dve(lambda e, r=r: e.tensor_tensor(r.t[:], r.t[:], lam.t[:, 2:3], ALU.mult), [r, lam], [r])
                    P.dve(lambda e, o=o, acc=acc, r=r, o1=o1, off=off: e.scalar_tensor_tensor(
                        o.t[:], acc.t[:, off:off + 128], r.t[:], o1.t[:], ALU.mult, ALU.add), [acc, r, o1], [o])
                    P.dve(lambda e, o=o: e.tensor_tensor(junkf.t[:], o.t[:], o.t[:], ALU.mult), [o], [junkf])
                    P.dve(lambda e, s_=s_: e.reduce_sum(s_.t[:], junkf.t[:], axis=AX.X), [junkf], [s_])
                    P.act(lambda e, s_=s_: e.activation(s_.t[:], s_.t[:], AF.Ln, bias=1e-5, scale=1.0 / 128), [s_], [s_])
                    P.act(lambda e, s_=s_: e.activation(s_.t[:], s_.t[:], AF.Exp, scale=-0.5), [s_], [s_])
                    P.dve(lambda e, ya=ya, o=o, s_=s_: e.scalar_tensor_tensor(
                        ya.t[:], o.t[:], s_.t[:], gsub.t[:], ALU.mult, ALU.mult), [o, s_, gsub], [ya])
                    tr(P, C, pTr.t[:, 0:128], ya.t[:], [ya], [pTr])
                    P.dve(lambda e, yT=yT, i=i: e.tensor_copy(yT.t[:, 128 * i:128 * i + 128], pTr.t[:, 0:128]), [pTr], [yT])
            if c == 1:
                P.dma('sp', C.YT[h, :, qb * 512:(qb + 1) * 512], yT.t[:], [yT], [], final=True)
        P.emit()
    if not getattr(C, 'skip_dil', False):
        pass_M_dil(nc, C, li)


def pass_M_dil(nc, C, li):
    DIL = (1, 4, 16)
    for ci, d in enumerate(DIL):
        L = S // d
        nb = L // 128
        P = Prog(nc)
        with ExitStack() as es:
            mask4 = P.sb(es, "D_mask", [128, 512], BF16)
            QTd = [P.sb(es, "D_QT%d" % j, [128, S], BF16) for j in range(2)]
            KZd = [[P.sb(es, "D_KZ%d_%d" % (j, hh), [128, S], BF16) for hh in range(2)] for j in range(2)]
            Vd = P.sb(es, "D_V", [128, NT, 8, 65], BF16)
            pts = [P.sb(es, "D_pt%d" % j, [128, 512], BF16) for j in range(4)]
            obs = [P.sb(es, "D_ob%d" % j, [128, 2, 65], F32) for j in range(4)]
            sts = [P.ps(es, "D_st%d" % j, [128, 512], F32) for j in range(3)]
            accs = [P.ps(es, "D_acc%d" % j, [128, 512], F32) for j in range(4)]
            for j in range(2):
                P.pool(lambda e, j=j: e.memset(KZd[j][0].t[64:128, :], 0.0), [], [KZd[j][0]])
                P.pool(lambda e, j=j: e.memset(KZd[j][1].t[0:64, :], 0.0), [], [KZd[j][1]])
            P.pool(lambda e: e.memset(mask4.t[:], 1.0), [], [mask4])
            for q4 in range(4):
                if q4 % 2 == 0:
                    P.pool(lambda e, q4=q4: e.affine_select(mask4.t[:, q4 * 128:(q4 + 1) * 128], mask4.t[:, q4 * 128:(q4 + 1) * 128],
                                                            pattern=[[1, 128]], compare_op=ALU.is_ge, fill=0.0, base=0,
                                                            channel_multiplier=-1), [mask4], [mask4])
                else:
                    P.pool(lambda e, q4=q4: e.affine_select(mask4.t[:, q4 * 128:(q4 + 1) * 128], mask4.t[:, q4 * 128:(q4 + 1) * 128],
                                                            pattern=[[-1, 128]], compare_op=ALU.is_ge, fill=0.0, base=0,
                                                            channel_multiplier=1), [mask4], [mask4])
            P.pool(lambda e: e.memset(Vd.t[:, :, :, 64:65], 1.0), [], [Vd])
            Vst = P.sb(es, "D_Vst", [128, NT, 512], BF16)
            vsrc = C.VS[:, 512:1024].rearrange("(n m r) c -> r m n c", m=128, r=d)
            for r in range(d):
                for n0 in range(0, nb, 4):
                    n1 = min(nb, n0 + 4)
                    P.dma('sp', Vst.t[:, r * nb + n0:r * nb + n1, :], vsrc[r][:, n0:n1, :], [], [Vst])
            for q4 in range(4):
                n0, n1 = q4 * NT // 4, (q4 + 1) * NT // 4
                eng = P.pool if q4 % 2 == 0 else P.act
                if q4 % 2 == 0:
                    P.pool(lambda e, n0=n0, n1=n1: e.tensor_copy(Vd.t[:, n0:n1, :, 0:64], Vst.t[:, n0:n1, :].rearrange("p n (h e) -> p n h e", h=8)),
                           [Vst], [Vd])
                else:
                    P.act(lambda e, n0=n0, n1=n1: e.copy(Vd.t[:, n0:n1, :, 0:64], Vst.t[:, n0:n1, :].rearrange("p n (h e) -> p n h e", h=8)),
                          [Vst], [Vd])
            dosrc = C.DO[ci].rearrange("(n m r) c -> r n m c", m=128, r=d)
            nob = 0
            steps = [(hp, r, n) for hp in range(4) for r in range(d) for n in range(nb)]
            hb = {}

            def hp_bufs(hp):
                if hp not in hb:
                    Q, K_ = QTd[hp % 2], KZd[hp % 2]
                    P.dma('sp', Q.t[:], C.QT[8 + hp], [], [Q])
                    P.dma('sp', K_[0].t[0:64, :], C.QT[12 + hp][0:64, :], [], [K_[0]])
                    P.dma('sp', K_[1].t[64:128, :], C.QT[12 + hp][64:128, :], [], [K_[1]])
                    hb[hp] = (Q, K_)
                return hb[hp]

            def emit_qk(t):
                hp, r, n = steps[t]
                Q, K_ = hp_bufs(hp)
                nq = 2 if n + 1 < nb else 1
                k0 = n * 128 * d + r
                st = sts[t % 3]
                for hh in range(2):
                    mm(P, st.t[:, hh * 256:hh * 256 + nq * 128], K_[hh].t[:, k0:k0 + 127 * d + 1:d],
                       Q.t[:, k0:k0 + (nq * 128 - 1) * d + 1:d], True, True, [K_[hh], Q], [st])

            LOOKD = 2
            for t in range(min(LOOKD, len(steps))):
                emit_qk(t)
            for t, (hp, r, n) in enumerate(steps):
                kb = r * nb + n
                nq = 2 if n + 1 < nb else 1
                pt = pts[t % 4]
                st = sts[t % 3]
                if nq == 2:
                    P.act(lambda e, pt=pt, st=st: e.activation(pt.t[:], st.t[:], AF.Exp), [st], [pt])
                else:
                    for hh in range(2):
                        P.act(lambda e, pt=pt, st=st, hh=hh: e.activation(pt.t[:, hh * 256:hh * 256 + 128],
                                                                      st.t[:, hh * 256:hh * 256 + 128], AF.Exp), [st], [pt])
                if nq == 2:
                    P.dve(lambda e, pt=pt: e.tensor_tensor(pt.t[:], pt.t[:], mask4.t[:], ALU.mult), [pt, mask4], [pt])
                else:
                    for hh in range(2):
                        P.dve(lambda e, pt=pt, hh=hh: e.tensor_tensor(pt.t[:, hh * 256:hh * 256 + 128], pt.t[:, hh * 256:hh * 256 + 128],
                                                                     mask4.t[:, 0:128], ALU.mult), [pt, mask4], [pt])
                if t + LOOKD < len(steps):
                    emit_qk(t + LOOKD)
                for qq in range(nq):
                    acc = accs[(n + qq) % 4]
                    first = (qq == 1 or n == 0)
                    for hh in range(2):
                        hidx = 2 * hp + hh
                        P.pe(lambda e, acc=acc, hh=hh, qq=qq, pt=pt, kb=kb, hidx=hidx, first=first: e.matmul(
                            acc.t[:, hh * 256:hh * 256 + 65], pt.t[:, hh * 256 + qq * 128:hh * 256 + qq * 128 + 128], Vd.t[:, kb, hidx, :],
                            start=(first and hh == 0), stop=(qq == 0), skip_group_check=True), [pt, Vd], [acc])
                ob = obs[nob % 4]
                nob += 1
                acc = accs[n % 4]
                src = acc.t[:, :].rearrange("p (h c) -> p h c", h=2)[:, :, 0:65]
                if nob % 2 == 0:
                    P.act(lambda e, ob=ob, src=src: e.copy(ob.t[:, :, :], src), [acc], [ob])
                else:
                    P.dve(lambda e, ob=ob, src=src: e.tensor_copy(ob.t[:, :, :], src), [acc], [ob])
                P.dma('sp', dosrc[r, n][:, hp * 130:(hp + 1) * 130], ob.t[:, :, :].rearrange("p a b -> p (a b)"), [ob], [], final=True)
            P.emit()
    P = Prog(nc)
    with ExitStack() as es:
        C.ident = P.sb(es, "G_id", [128, 128], BF16)
        dts = [P.sb(es, "G_d%d" % j, [128, 3, 520], F32) for j in range(2)]
        rd = [P.sb(es, "G_rd%d" % j, [128, 8], F32) for j in range(2)]
        yb = [P.sb(es, "G_yb%d" % j, [128, 512], BF16) for j in range(2)]
        yT = [P.sb(es, "G_yT%d" % j, [128, 4, 128], BF16) for j in range(2)]
        pTr = [P.ps(es, "G_pT%d" % j, [128, 1024], BF16) for j in range(2)]
        make_identity(P, C.ident)
        for t in range(NT):
            dt_ = dts[t % 2]
            j = t % 2
            P.dma('sp', dt_.t[:], C.DO[:, t * 128:(t + 1) * 128, :].rearrange("c p f -> p c f"), [], [dt_])
            P.dve(lambda e, dt_=dt_: e.tensor_tensor(dt_.t[:, 0, :], dt_.t[:, 0, :], dt_.t[:, 1, :], ALU.add), [dt_], [dt_])
            P.dve(lambda e, dt_=dt_: e.tensor_tensor(dt_.t[:, 0, :], dt_.t[:, 0, :], dt_.t[:, 2, :], ALU.add), [dt_], [dt_])
            v3 = dt_.t[:, 0, :].rearrange("p (h e) -> p h e", h=8)
            P.dve(lambda e, v3=v3, j=j: e.reciprocal(rd[j].t[:, :], v3[:, :, 64]), [dt_], [rd[j]])
            P.dve(lambda e, v3=v3, j=j: e.tensor_tensor(yb[j].t[:, :].rearrange("p (h e) -> p h e", h=8), v3[:, :, 0:64],
                                                       rd[j].t[:, :].unsqueeze(2).to_broadcast([128, 8, 64]), ALU.mult),
                  [dt_, rd[j]], [yb[j]])
            for k in range(4):
                tr(P, C, pTr[j].t[:, k * 128:(k + 1) * 128], yb[j].t[:, k * 128:(k + 1) * 128], [yb[j]], [pTr[j]])
            P.act(lambda e, j=j: e.copy(yT[j].t[:, :, :], pTr[j].t[:, 0:512].rearrange("p (k t) -> p k t", k=4)), [pTr[j]], [yT[j]])
            P.dma('sp', C.YT[4:8, :, t * 128:(t + 1) * 128].rearrange("k p t -> p k t"), yT[j].t[:, :, :], [yT[j]], [], final=True)
        P.emit()


def make_identity_f32(P, ident):
    t = ident.t
    P.pool(lambda e: e.memset(t[:], 1.0), [], [ident])
    P.pool(lambda e: e.affine_select(t[:], t[:], pattern=[[-1, 128]], compare_op=ALU.is_equal,
                                    fill=0.0, base=0, channel_multiplier=1), [ident], [ident])


def pass_M_odd(nc, C, li):
    import math
    o = li // 2
    TB = 512
    NB = S // TB
    TWO_PI = 2.0 * math.pi
    P = Prog(nc)
    with ExitStack() as es:
        identF = P.sb(es, "O_idF", [128, 128], F32)
        C.ident = P.sb(es, "O_id", [128, 128], BF16)
        stg = [P.sb(es, "O_stg%d" % j, [32, 128], F32) for j in range(2)]
        lr = P.sb(es, "O_lr", [128, 16], F32)
        lim = P.sb(es, "O_li", [128, 16], F32)
        dtv = P.sb(es, "O_dt", [128, 16], F32)
        ld2 = P.sb(es, "O_ld2", [16, 2], F32)
        th = P.sb(es, "O_th", [128, 16], F32)
        mag = P.sb(es, "O_mag", [128, 16], F32)
        ph = [P.sb(es, "O_ph%d" % j, [128, 16], F32) for j in range(2)]
        tq = P.sb(es, "O_tq", [128, 16], F32)
        cs = P.sb(es, "O_cs", [128, 16], F32)
        sn = P.sb(es, "O_sn", [128, 16], F32)
        w_ = [P.sb(es, "O_w%d" % j, [128, 16], F32) for j in range(6)]
        cre = P.sb(es, "O_cre", [128, 16], F32)
        cim = P.sb(es, "O_cim", [128, 16], F32)
        dT = P.sb(es, "O_d", [128, 4], F32)
        cwv = P.sb(es, "O_cw", [128, 12], F32)
        CTt = P.sb(es, "O_CT", [128, 16, TB], F32)
        STt = P.sb(es, "O_ST", [128, 16, TB], F32)
        tA = P.sb(es, "O_tA", [128, 8, 256], F32)
        tB = P.sb(es, "O_tB", [128, 8, 256], F32)
        Bst = [P.sb(es, "O_Bst%d" % j, [128, 16, 16], F32) for j in range(2)]
        bb = [P.sb(es, "O_bb%d" % j, [128, 16, 16], F32) for j in range(2)]
        bt_ = [P.sb(es, "O_bt%d" % j, [128, 16, 16], F32) for j in range(2)]
        CC = [P.sb(es, "O_CC%d" % j, [128, 4, 64], F32) for j in range(2)]
        maskB = [P.sb(es, "O_mB%d" % j, [128, 128], F32) for j in range(4)]
        maskC = [P.sb(es, "O_mC%d" % j, [128, 128], F32) for j in range(4)]
        BF = [P.sb(es, "O_BF%d" % j, [128, 128], BF16) for j in range(2)]
        BT = [P.sb(es, "O_BT%d" % j, [128, 16, 128], BF16) for j in range(2)]
        CM = [P.sb(es, "O_CM%d" % j, [128, 16, 128], BF16) for j in range(2)]
        wglu = P.sb(es, "O_wglu", [128, 4, 512], BF16)
        xprev = [P.sb(es, "O_xp%d" % j, [128, 16], F32) for j in range(2)]
        ins_ = [[P.sb(es, "O_in%d_%d" % (q, j), [128, 4, TB], BF16) for j in range(1)] for q in range(4)]
        T = [P.sb(es, "O_T%d" % j, [128, TB], F32) for j in range(4)]
        btr = P.sb(es, "O_btr", [128, TB], F32)
        bti = P.sb(es, "O_bti", [128, TB], F32)
        zr = P.sb(es, "O_zr", [128, TB], F32)
        zi = P.sb(es, "O_zi", [128, TB], F32)
        xrf = [P.sb(es, "O_xrf%d" % j, [128, TB], F32) for j in range(2)]
        xif = [P.sb(es, "O_xif%d" % j, [128, TB], F32) for j in range(2)]
        xrb = [P.sb(es, "O_xrb%d" % j, [128, TB], BF16) for j in range(2)]
        xib = [P.sb(es, "O_xib%d" % j, [128, TB], BF16) for j in range(2)]
        yf = P.sb(es, "O_yf", [128, TB], F32)
        g1 = P.sb(es, "O_g1", [128, TB], F32)
        g2 = P.sb(es, "O_g2", [128, TB], F32)
        ygf = P.sb(es, "O_ygf", [128, 4, TB], F32)
        ygb = P.sb(es, "O_ygb", [128, 4, TB], BF16)
        sg2 = [P.sb(es, "O_sg%d" % j, [128, TB], F32) for j in range(2)]
        ycT = [P.sb(es, "O_yc%d" % j, [128, TB], BF16) for j in range(2)]
        zb = [P.sb(es, "O_zb%d" % j, [128, TB + 2], F32) for j in range(4)]
        yv = [P.sb(es, "O_yv%d" % j, [128, TB], F32) for j in range(1)]
        ydT = [P.sb(es, "O_yd%d" % j, [128, TB], BF16) for j in range(2)]
        ytmps = [P.sb(es, "O_ytmp%d" % j, [128, TB], F32) for j in range(2)]
        pmisc = P.ps(es, "O_pm", [128, 512], F32)
        pTr = P.ps(es, "O_pT", [128, 1024], BF16)
        pb = [P.ps(es, "O_pb%d" % j, [128, 512], F32) for j in range(4)]
        py = [P.ps(es, "O_py%d" % j, [128, 512], F32) for j in range(2)]

        make_identity_f32(P, identF)
        make_identity(P, C.ident)
        load_w_bf16(P, wglu, C.ssm_w_glu[o], 4)
        nstg = [0]

        def load_T(dst_ap, dst_buf, src_ap, n):
            st_ = stg[nstg[0] % 2]
            nstg[0] += 1
            P.dma('sp', st_.t[0:n, :], src_ap, [], [st_])
            P.pe(lambda e: e.matmul(pmisc.t[:, 0:n], st_.t[0:n, :], identF.t[0:n, 0:n], start=True, stop=True),
                 [st_, identF], [pmisc])
            P.dve(lambda e: e.tensor_copy(dst_ap, pmisc.t[:, 0:n]), [pmisc], [dst_buf])

        load_T(lr.t[:, :], lr, C.ssm_lambda_re[o].rearrange("(j a) p -> j (a p)", a=2), 16)
        load_T(lim.t[:, :], lim, C.ssm_lambda_im[o].rearrange("(j a) p -> j (a p)", a=2), 16)
        load_T(dT.t[:, :], dT, C.ssm_d[o].rearrange("(t a) c -> t (a c)", a=8), 4)
        load_T(cwv.t[:, :], cwv, C.conv_w[o].rearrange("k (t c) -> (k t) c", c=128), 12)
        P.dma('sp', ld2.t[:, :], C.ssm_log_dt[o].rearrange("(j a) -> j a", a=2), [], [ld2])
        st_ = stg[nstg[0] % 2]
        nstg[0] += 1
        P.dve(lambda e: e.tensor_copy(st_.t[0:16, :].rearrange("j (a p) -> j a p", a=2),
                                      ld2.t[:, :].unsqueeze(2).to_broadcast([16, 2, 64])), [ld2], [st_])
        P.pe(lambda e: e.matmul(pmisc.t[:, 0:16], st_.t[0:16, :], identF.t[0:16, 0:16], start=True, stop=True), [st_, identF], [pmisc])
        P.act(lambda e: e.activation(dtv.t[:], pmisc.t[:, 0:16], AF.Exp), [pmisc], [dtv])
        P.dve(lambda e: e.tensor_tensor(th.t[:], lim.t[:], dtv.t[:], ALU.mult), [lim, dtv], [th])
        P.dve(lambda e: e.tensor_tensor(mag.t[:], lr.t[:], dtv.t[:], ALU.mult), [lr, dtv], [mag])
        P.act(lambda e: e.activation(mag.t[:], mag.t[:], AF.Exp), [mag], [mag])

        def range_reduce(dst, shift):
            P.dve(lambda e: e.tensor_scalar(dst.t[:], th.t[:], shift, None, ALU.add), [th], [dst])
            for m_ in range(1, 21):
                thr = (2 * m_ - 1) * math.pi - shift
                P.dve(lambda e, thr=thr: e.tensor_scalar(tq.t[:], th.t[:], thr, -TWO_PI, ALU.is_ge, ALU.mult), [th], [tq])
                P.dve(lambda e: e.tensor_tensor(dst.t[:], dst.t[:], tq.t[:], ALU.add), [dst, tq], [dst])

        range_reduce(ph[0], 0.0)
        range_reduce(ph[1], math.pi / 2)
        P.act(lambda e: e.activation(sn.t[:], ph[0].t[:], AF.Sin), [ph[0]], [sn])
        P.act(lambda e: e.activation(cs.t[:], ph[1].t[:], AF.Sin), [ph[1]], [cs])
        abr, abi, den, nr, u1, u2 = w_
        TT = lambda out, a, b, op: P.dve(lambda e: e.tensor_tensor(out.t[:], a.t[:], b.t[:], op), [a, b], [out])
        TT(abr, mag, cs, ALU.mult)
        TT(abi, mag, sn, ALU.mult)
        TT(den, lr, lr, ALU.mult)
        TT(u1, lim, lim, ALU.mult)
        TT(den, den, u1, ALU.add)
        P.dve(lambda e: e.reciprocal(den.t[:], den.t[:]), [den], [den])
        P.dve(lambda e: e.tensor_scalar(nr.t[:], abr.t[:], -1.0, None, ALU.add), [abr], [nr])
        TT(u1, nr, lr, ALU.mult)
        TT(u2, abi, lim, ALU.mult)
        TT(u1, u1, u2, ALU.add)
        TT(cre, u1, den, ALU.mult)
        TT(u1, abi, lr, ALU.mult)
        TT(u2, nr, lim, ALU.mult)
        TT(u1, u1, u2, ALU.subtract)
        TT(cim, u1, den, ALU.mult)
        P.dve(lambda e: e.tensor_copy(CTt.t[:, :, 0], cs.t[:, :]), [cs], [CTt])
        P.dve(lambda e: e.tensor_copy(STt.t[:, :, 0], sn.t[:, :]), [sn], [STt])
        w = 1
        while w < TB:
            for jh in range(2):
                js = slice(jh * 8, jh * 8 + 8)
                cwb = CTt.t[:, js, w - 1].unsqueeze(2).to_broadcast([128, 8, w])
                swb = STt.t[:, js, w - 1].unsqueeze(2).to_broadcast([128, 8, w])
                c0, s0 = CTt.t[:, js, 0:w], STt.t[:, js, 0:w]
                c1, s1 = CTt.t[:, js, w:2 * w], STt.t[:, js, w:2 * w]
                a_, b_ = tA.t[:, :, 0:w], tB.t[:, :, 0:w]
                P.dve(lambda e, a_=a_, c0=c0, cwb=cwb: e.tensor_tensor(a_, c0, cwb, ALU.mult), [CTt], [tA])
                P.pool(lambda e, b_=b_, s0=s0, swb=swb: e.tensor_tensor(b_, s0, swb, ALU.mult), [STt, CTt], [tB])
                P.dve(lambda e, a_=a_, b_=b_, c1=c1: e.tensor_tensor(c1, a_, b_, ALU.subtract), [tA, tB], [CTt])
                P.dve(lambda e, a_=a_, s0=s0, cwb=cwb: e.tensor_tensor(a_, s0, cwb, ALU.mult), [STt, CTt], [tA])
                P.pool(lambda e, b_=b_, c0=c0, swb=swb: e.tensor_tensor(b_, c0, swb, ALU.mult), [CTt, STt], [tB])
                P.dve(lambda e, a_=a_, b_=b_, s1=s1: e.tensor_tensor(s1, a_, b_, ALU.add), [tA, tB], [STt])
            w *= 2
        for m_ in range(4):
            P.pool(lambda e, m_=m_: e.memset(maskB[m_].t[:], 1.0), [], [maskB[m_]])
            P.pool(lambda e, m_=m_: e.memset(maskC[m_].t[:], 1.0), [], [maskC[m_]])
            for a in range(2):
                blk = 16 * (2 * m_ + a)
                tB_ = maskB[m_].t[a * 64:(a + 1) * 64, :]
                P.pool(lambda e, tB_=tB_, blk=blk: e.affine_select(tB_, tB_, pattern=[[1, 128]], compare_op=ALU.is_ge, fill=0.0,
                                                                base=-blk, channel_multiplier=0), [maskB[m_]], [maskB[m_]])
                P.pool(lambda e, tB_=tB_, blk=blk: e.affine_select(tB_, tB_, pattern=[[-1, 128]], compare_op=ALU.is_ge, fill=0.0,
                                                                base=blk + 15, channel_multiplier=0), [maskB[m_]], [maskB[m_]])
                tC_ = maskC[m_].t[:, a * 64:(a + 1) * 64]
                P.pool(lambda e, tC_=tC_, blk=blk: e.affine_select(tC_, tC_, pattern=[[0, 64]], compare_op=ALU.is_ge, fill=0.0,
                                                                base=-blk, channel_multiplier=1), [maskC[m_]], [maskC[m_]])
                P.pool(lambda e, tC_=tC_, blk=blk: e.affine_select(tC_, tC_, pattern=[[0, 64]], compare_op=ALU.is_ge, fill=0.0,
                                                                base=blk + 15, channel_multiplier=-1), [maskC[m_]], [maskC[m_]])
        for ri, nm in enumerate(['ssm_b_re', 'ssm_b_im']):
            src = getattr(C, nm)[o].rearrange("(j a) p c -> (a p) j c", a=2)
            for j0 in range(0, 16, 4):
                P.dma('sp', Bst[ri].t[:, j0:j0 + 4, :], src[:, j0:j0 + 4, :], [], [Bst[ri]])
        creb = cre.t[:, :].unsqueeze(2).to_broadcast([128, 16, 16])
        cimb = cim.t[:, :].unsqueeze(2).to_broadcast([128, 16, 16])
        P.dve(lambda e: e.tensor_tensor(bb[0].t[:], Bst[0].t[:], creb, ALU.mult), [Bst[0], cre], [bb[0]])
        P.dve(lambda e: e.tensor_tensor(bt_[0].t[:], Bst[1].t[:], cimb, ALU.mult), [Bst[1], cim], [bt_[0]])
        P.dve(lambda e: e.tensor_tensor(bb[0].t[:], bb[0].t[:], bt_[0].t[:], ALU.subtract), [bb[0], bt_[0]], [bb[0]])
        P.dve(lambda e: e.tensor_tensor(bb[1].t[:], Bst[1].t[:], creb, ALU.mult), [Bst[1], cre], [bb[1]])
        P.dve(lambda e: e.tensor_tensor(bt_[1].t[:], Bst[0].t[:], cimb, ALU.mult), [Bst[0], cim], [bt_[1]])
        P.dve(lambda e: e.tensor_tensor(bb[1].t[:], bb[1].t[:], bt_[1].t[:], ALU.add), [bb[1], bt_[1]], [bb[1]])
        for ri, nm in enumerate(['ssm_c_re', 'ssm_c_im']):
            src = getattr(C, nm)[o].rearrange("(t g) c p -> (g c) t p", g=8)
            P.dma('sp', CC[ri].t[:, :, :], src, [], [CC[ri]])
        nbf = 0
        for j in range(16):
            for ri in range(2):
                bf = BF[nbf % 2]
                nbf += 1
                P.dve(lambda e, bf=bf, j=j, ri=ri: e.tensor_tensor(
                    bf.t[:, :].rearrange("q (k c) -> q k c", k=8), maskB[j % 4].t[:, :].rearrange("q (k c) -> q k c", k=8),
                    bb[ri].t[:, j, :].unsqueeze(1).to_broadcast([128, 8, 16]), ALU.mult), [maskB[j % 4], bb[ri]], [bf])
                tr(P, C, pTr.t[:, 0:128], bf.t[:], [bf], [pTr])
                P.act(lambda e, j=j, ri=ri: e.copy(BT[ri].t[:, j, :], pTr.t[:, 0:128]), [pTr], [BT[ri]])
            for ri in range(2):
                bf = BF[nbf % 2]
                nbf += 1
                sgn = 1.0 if ri == 0 else -1.0
                P.dve(lambda e, bf=bf, j=j, ri=ri, sgn=sgn: e.scalar_tensor_tensor(
                    bf.t[:, :].rearrange("q (a p) -> q a p", a=2), maskC[j % 4].t[:, :].rearrange("q (a p) -> q a p", a=2), sgn,
                    CC[ri].t[:, j // 4, :].unsqueeze(1).to_broadcast([128, 2, 64]), ALU.mult, ALU.mult), [maskC[j % 4], CC[ri]], [bf])
                tr(P, C, pTr.t[:, 0:128], bf.t[:], [bf], [pTr])
                P.act(lambda e, j=j, ri=ri: e.copy(CM[ri].t[:, j, :], pTr.t[:, 0:128]), [pTr], [CM[ri]])
        for q in range(2):
            P.pool(lambda e, q=q: e.memset(xprev[q].t[:], 0.0), [], [xprev[q]])
        for ct in range(4):
            P.pool(lambda e, ct=ct: e.memset(zb[ct].t[:, TB:TB + 2], 0.0), [], [zb[ct]])

        def loads(b):
            for q in range(4):
                t_ = ins_[q][0]
                P.dma('sp', t_.t[:, :, :], C.QT[4 * q:4 * q + 4, :, b * TB:(b + 1) * TB].rearrange("k p t -> p k t"), [], [t_])

        nx = 0
        nyc = 0
        for b in range(NB):
            loads(b)
            uT, gbT, gcT, xtT = [ins_[q][0] for q in range(4)]
            for ct in range(4):
                z = zb[ct]
                y_ = yv[0]
                yd = ydT[ct % 2]
                P.act(lambda e, z=z: e.copy(z.t[:, 0:2], z.t[:, TB:TB + 2]), [z], [z])
                P.dve(lambda e, z=z, ct=ct, gcT=gcT, xtT=xtT: e.tensor_tensor(z.t[:, 2:TB + 2], gcT.t[:, ct, :], xtT.t[:, ct, :], ALU.mult),
                      [gcT, xtT, z], [z])
                P.act(lambda e, z=z, y_=y_, ct=ct: e.activation(y_.t[:], z.t[:, 2:TB + 2], AF.Copy, scale=cwv.t[:, ct:ct + 1]), [z, cwv], [y_])
                for kk in (1, 2):
                    yt_ = ytmps[kk - 1]
                    P.act(lambda e, z=z, ct=ct, kk=kk, yt_=yt_: e.activation(yt_.t[:], z.t[:, 2 - kk:TB + 2 - kk], AF.Copy,
                                                                            scale=cwv.t[:, 4 * kk + ct:4 * kk + ct + 1]), [z, cwv], [yt_])
                    P.dve(lambda e, y_=y_, yt_=yt_: e.tensor_tensor(y_.t[:], y_.t[:], yt_.t[:], ALU.add), [y_, yt_], [y_])
                P.dve(lambda e, yd=yd, y_=y_, ct=ct, gbT=gbT: e.tensor_tensor(yd.t[:], gbT.t[:, ct, :], y_.t[:], ALU.mult), [gbT, y_], [yd])
                P.dma('sp', C.YT[4 + ct, :, b * TB:(b + 1) * TB], yd.t[:], [yd], [], final=True)
            for j in range(16):
                ct = j // 4
                pbr, pbi = pb[(j % 2) * 2], pb[(j % 2) * 2 + 1]
                pyc = py[ct % 2]
                mm(P, pbr.t[:], BT[0].t[:, j, :], uT.t[:, ct, :], True, True, [BT[0], uT], [pbr])
                mm(P, pbi.t[:], BT[1].t[:, j, :], uT.t[:, ct, :], True, True, [BT[1], uT], [pbi])
                cj, sj = CTt.t[:, j, :], STt.t[:, j, :]
                V = lambda out, a_ap, b_ap, op, rd, wr: P.dve(lambda e: e.tensor_tensor(out, a_ap, b_ap, op), rd, wr)
                V(T[0].t[:], cj, pbr.t[:], ALU.mult, [CTt, pbr], [T[0]])
                V(T[1].t[:], sj, pbi.t[:], ALU.mult, [STt, pbi], [T[1]])
                V(btr.t[:], T[0].t[:], T[1].t[:], ALU.add, [T[0], T[1]], [btr])
                V(T[2].t[:], cj, pbi.t[:], ALU.mult, [CTt, pbi], [T[2]])
                V(T[3].t[:], sj, pbr.t[:], ALU.mult, [STt, pbr], [T[3]])
                V(bti.t[:], T[2].t[:], T[3].t[:], ALU.subtract, [T[2], T[3]], [bti])
                rdec = mag.t[:, j:j + 1].to_broadcast([128, TB])
                P.dve(lambda e, rdec=rdec, j=j: e.tensor_tensor_scan(zr.t[:], rdec, btr.t[:], xprev[0].t[:, j:j + 1], ALU.mult, ALU.add),
                      [mag, btr, xprev[0]], [zr])
                P.dve(lambda e, rdec=rdec, j=j: e.tensor_tensor_scan(zi.t[:], rdec, bti.t[:], xprev[1].t[:, j:j + 1], ALU.mult, ALU.add),
                      [mag, bti, xprev[1]], [zi])
                xr_, xi_ = xrf[nx % 2], xif[nx % 2]
                xrb_, xib_ = xrb[nx % 2], xib[nx % 2]
                nx += 1
                V(T[0].t[:], cj, zr.t[:], ALU.mult, [CTt, zr], [T[0]])
                V(T[1].t[:], sj, zi.t[:], ALU.mult, [STt, zi], [T[1]])
                V(xr_.t[:], T[0].t[:], T[1].t[:], ALU.subtract, [T[0], T[1]], [xr_])
                V(T[2].t[:], sj, zr.t[:], ALU.mult, [STt, zr], [T[2]])
                V(T[3].t[:], cj, zi.t[:], ALU.mult, [CTt, zi], [T[3]])
                V(xi_.t[:], T[2].t[:], T[3].t[:], ALU.add, [T[2], T[3]], [xi_])
                P.act(lambda e, xr_=xr_, j=j: e.copy(xprev[0].t[:, j:j + 1], xr_.t[:, TB - 1:TB]), [xr_], [xprev[0]])
                P.act(lambda e, xi_=xi_, j=j: e.copy(xprev[1].t[:, j:j + 1], xi_.t[:, TB - 1:TB]), [xi_], [xprev[1]])
                P.act(lambda e, xr_=xr_, xrb_=xrb_: e.copy(xrb_.t[:], xr_.t[:]), [xr_], [xrb_])
                P.act(lambda e, xi_=xi_, xib_=xib_: e.copy(xib_.t[:], xi_.t[:]), [xi_], [xib_])
                mm(P, pyc.t[:], CM[0].t[:, j, :], xrb_.t[:], j % 4 == 0, False, [CM[0], xrb_], [pyc])
                mm(P, pyc.t[:], CM[1].t[:, j, :], xib_.t[:], False, j % 4 == 3, [CM[1], xib_], [pyc])
                if j % 4 == 3:
                    P.dve(lambda e, ct=ct, pyc=pyc, uT=uT: e.scalar_tensor_tensor(yf.t[:], uT.t[:, ct, :], dT.t[:, ct:ct + 1], pyc.t[:], ALU.mult, ALU.add),
                          [uT, dT, pyc], [yf])
                    P.act(lambda e: e.activation(g1.t[:], yf.t[:], AF.Square), [yf], [g1])
                    P.dve(lambda e: e.tensor_scalar(g1.t[:], g1.t[:], 0.044715, 1.0, ALU.mult, ALU.add), [g1], [g1])
                    P.dve(lambda e: e.tensor_tensor(g1.t[:], g1.t[:], yf.t[:], ALU.mult), [g1, yf], [g1])
                    P.act(lambda e: e.activation(g2.t[:], g1.t[:], AF.Sigmoid, scale=1.5957691216057308), [g1], [g2])
                    P.dve(lambda e, ct=ct: e.tensor_tensor(ygf.t[:, ct, :], yf.t[:], g2.t[:], ALU.mult), [yf, g2], [ygf])
                    P.act(lambda e, ct=ct: e.copy(ygb.t[:, ct, :], ygf.t[:, ct, :]), [ygf], [ygb])
            for co in range(4):
                pg_ = pb[co]
                for k in range(4):
                    mm(P, pg_.t[:], wglu.t[:, k, co * 128:(co + 1) * 128], ygb.t[:, k, :], k == 0, k == 3, [wglu, ygb], [pg_])
                sg_ = sg2[co % 2]
                yc = ycT[nyc % 2]
                nyc += 1
                P.act(lambda e, sg_=sg_, pg_=pg_: e.activation(sg_.t[:], pg_.t[:], AF.Sigmoid), [pg_], [sg_])
                P.dve(lambda e, yc=yc, sg_=sg_, co=co: e.tensor_tensor(yc.t[:], ygf.t[:, co, :], sg_.t[:], ALU.mult), [ygf, sg_], [yc])
                P.dma('sp', C.YT[co, :, b * TB:(b + 1) * TB], yc.t[:], [yc], [], final=True)
        P.emit()


SKIP_DIL = False
WEIGHTS = [
    ('norm_mix', [4, 1024]), ('norm_mlp', [4, 1024]), ('norm_ple', [4, 1024]),
    ('w_mlp_in', [4, 1024, 4096]), ('w_mlp_out', [4, 4096, 1024]), ('w_ple_proj', [4, 256, 1024]),
    ('w_ple_gate', [4, 1024, 1024]), ('attn_w_in', [2, 1024, 3072]), ('attn_w_out', [2, 1024, 1024]),
    ('diff_lq1', [2, 64]), ('diff_lk1', [2, 64]), ('diff_lq2', [2, 64]), ('diff_lk2', [2, 64]),
    ('diff_sub_gain', [2, 128]), ('rc_w_in', [2, 1024, 2048]), ('rc_w_out', [2, 1024, 1024]),
    ('ssm_lambda_re', [2, 32, 64]), ('ssm_lambda_im', [2, 32, 64]), ('ssm_log_dt', [2, 32]),
    ('ssm_b_re', [2, 32, 64, 16]), ('ssm_b_im', [2, 32, 64, 16]), ('ssm_c_re', [2, 32, 16, 64]),
    ('ssm_c_im', [2, 32, 16, 64]), ('ssm_d', [2, 32, 16]), ('ssm_w_glu', [2, 512, 512]),
    ('conv_w', [2, 3, 512]), ('norm_final', [1024]),
]


def build_nc(passes=None, ext=()):
    nc = bass.Bass("TRN2", target_bir_lowering=False)
    C = Ctx()
    C.skip_dil = SKIP_DIL
    C.x = nc.dram_tensor("x", [S, D], F32, kind="ExternalInput").ap()
    C.p = nc.dram_tensor("p", [4, S, 256], F32, kind="ExternalInput").ap()
    for name, shp in WEIGHTS:
        setattr(C, name, nc.dram_tensor(name, shp, F32, kind="ExternalInput").ap())
    C.out = nc.dram_tensor("out", [S, D], F32, kind="ExternalOutput").ap()

    def scratch(name, shp, dt):
        kind = "Internal"
        for nm, k in ext:
            if nm == name:
                kind = k
        return nc.dram_tensor(name, shp, dt, kind=kind).ap()

    C.QT = scratch("QT", [16, 128, S], BF16)
    C.VS = scratch("VS", [S, D], BF16)
    C.YT = scratch("YT", [8, 128, S], BF16)
    C.DO = scratch("DO", [3, S, 8 * 65], F32)
    allp = []
    for li in range(4):
        allp += ["A%d" % li, "M%d" % li, "X%d" % li, "Y%d" % li]
    if passes is None:
        passes = allp
    for pn in passes:
        li = int(pn[1])
        if pn[0] == 'A':
            pass_A(nc, C, li, C.x if li == 0 else C.out)
        elif pn[0] == 'M':
            if li % 2 == 0:
                pass_M_even(nc, C, li)
            else:
                pass_M_odd(nc, C, li)
        elif pn[0] == 'X':
            pass_X(nc, C, li)
        elif pn[0] == 'Y':
            pass_Y(nc, C, li, last=(li == 3))
        elif pn[0] == 'C':
            pass_copy(nc, C)
    return nc


def pass_copy(nc, C):
    P = Prog(nc)
    with ExitStack() as es:
        hts = [P.sb(es, "C_h%d" % j, [128, D], F32) for j in range(4)]
        for t in range(NT):
            ht = hts[t % 4]
            P.dma('sp', ht.t[:], C.x[t * 128:(t + 1) * 128, :], [], [ht])
            P.dma('sp', C.out[t * 128:(t + 1) * 128, :], ht.t[:], [ht], [], final=True)
        P.emit()


def kernel(**inputs):
    nc = build_nc()
    shared = {k: np.ascontiguousarray(np.asarray(inputs[k], dtype=np.float32)) for k, _ in WEIGHTS}
    x = np.asarray(inputs['x'], dtype=np.float32)
    p = np.asarray(inputs['p'], dtype=np.float32)
    in_maps = []
    for b in range(8):
        m = dict(shared)
        m['x'] = np.ascontiguousarray(x[b])
        m['p'] = np.ascontiguousarray(p[:, b])
        in_maps.append(m)
    res = run_bass_kernel_spmd(nc, in_maps, core_ids=list(range(8)))
    return np.stack([np.asarray(r['out'], dtype=np.float32) for r in res.results], axis=0)
```
